# Optimizing a Trainium2 kernel written in Bass

```python
import jax, jax.numpy as jnp
from jax import lax
import numpy as np

D_MODEL = 1024
BATCH = 32
SEQ = 256
DEPTH = 2
DEC_BATCH = 8
DEC_SEQ = 4096
PAST_LEN = 256

GRID_W = 64
N_EVEN = (DEPTH + 1) // 2
N_ODD = DEPTH // 2
EPS = 1e-6
MLA_HEADS = 4
Q_LORA = 256
KV_LORA = 128
QK_NOPE = 128
QK_ROPE = 64
V_DIM = 128
MLA_WIDTH = MLA_HEADS * V_DIM
ROPE_THETA = 10000.0
Q_BLOCK = 128
POOL_WINDOWS = (2, 4, 8, 16)
POOL_GROUP = 128
POOL_WIDTH = len(POOL_WINDOWS) * POOL_GROUP
MIX_WIDTH_AB = MLA_WIDTH + POOL_WIDTH
SPLITS_AB = (Q_LORA, Q_LORA + KV_LORA, Q_LORA + KV_LORA + QK_ROPE,
             Q_LORA + KV_LORA + QK_ROPE + MLA_WIDTH,
             Q_LORA + KV_LORA + QK_ROPE + MLA_WIDTH + POOL_WIDTH)
IN_AB = Q_LORA + KV_LORA + QK_ROPE + MLA_WIDTH + 2 * POOL_WIDTH
CHUNK = 128
SGU_GROUPS = 4
C_WIDTH = D_MODEL
SGU_GROUP_DIM = C_WIDTH // SGU_GROUPS
IN_C = 3 * C_WIDTH

kernel_name = "hybrid_mla_pool_gmlp_diffusion_step"


def rmsnorm(x, g):
    xf = x.astype(jnp.float32)
    y = xf * lax.rsqrt(jnp.mean(xf * xf, axis=-1, keepdims=True) + EPS)
    return (y * g.astype(jnp.float32)).astype(x.dtype)


def layernorm(x, g, b):
    xf = x.astype(jnp.float32)
    mu = jnp.mean(xf, axis=-1, keepdims=True)
    var = jnp.mean(jnp.square(xf - mu), axis=-1, keepdims=True)
    y = (xf - mu) * lax.rsqrt(var + EPS)
    return (y * g.astype(jnp.float32) + b.astype(jnp.float32)).astype(x.dtype)


def rope_1d(x, pos):
    half = x.shape[-1] // 2
    inv = ROPE_THETA ** (-jnp.arange(half, dtype=jnp.float32) / half)
    ang = pos.astype(jnp.float32)[:, None] * inv
    ang = ang.reshape((ang.shape[0],) + (1,) * (x.ndim - 3) + (half,))
    cos, sin = jnp.cos(ang), jnp.sin(ang)
    xf = x.astype(jnp.float32)
    x1, x2 = xf[..., :half], xf[..., half:]
    return jnp.concatenate([x1 * cos - x2 * sin, x1 * sin + x2 * cos], axis=-1).astype(x.dtype)


def axial_rope(x, rows):
    row = jnp.repeat(jnp.arange(rows), GRID_W)
    col = jnp.tile(jnp.arange(GRID_W), rows)
    a = QK_ROPE // 2
    return jnp.concatenate([rope_1d(x[..., :a], row), rope_1d(x[..., a:], col)], axis=-1)


def mla_attend(q_nope, q_pe, k_nope, k_pe, v):
    B, Tq, H, _ = q_nope.shape
    nb = Tq // Q_BLOCK
    sm_scale = (QK_NOPE + QK_ROPE) ** -0.5

    def block(args):
        qn, qp = args
        s = (jnp.einsum('bqhd,bkhd->bhqk', qn, k_nope, preferred_element_type=jnp.float32)
             + jnp.einsum('bqhr,bkr->bhqk', qp, k_pe, preferred_element_type=jnp.float32))
        p = jax.nn.softmax(s * sm_scale, axis=-1).astype(v.dtype)
        return jnp.einsum('bhqk,bkhd->bqhd', p, v)

    qn_b = q_nope.reshape(B, nb, Q_BLOCK, H, QK_NOPE).transpose(1, 0, 2, 3, 4)
    qp_b = q_pe.reshape(B, nb, Q_BLOCK, H, QK_ROPE).transpose(1, 0, 2, 3, 4)
    o = lax.map(block, (qn_b, qp_b))
    return o.transpose(1, 0, 2, 3, 4).reshape(B, Tq, H * V_DIM)


def multiscale_pool(u, w_pool, pool_scale):
    B, T, _ = u.shape
    uf = u.astype(jnp.float32)
    cs = jnp.concatenate([jnp.zeros((B, 1, POOL_WIDTH), jnp.float32), jnp.cumsum(uf, axis=1)], axis=1)
    t = jnp.arange(T)
    outs = []
    for g, w in enumerate(POOL_WINDOWS):
        lo = jnp.clip(t - w // 2, 0, T)
        hi = jnp.clip(t + w - w // 2, 0, T)
        sl = slice(g * POOL_GROUP, (g + 1) * POOL_GROUP)
        csg = cs[:, :, sl]
        mean = (csg[:, hi] - csg[:, lo]) / (hi - lo).astype(jnp.float32)[None, :, None]
        d = (mean - uf[:, :, sl]).astype(u.dtype)
        outs.append(d @ w_pool[g])
    return jnp.concatenate(outs, axis=-1) * pool_scale


def mixer_ap(h, p, ctx_ckv=None, ctx_kpe=None):
    w_in, q_norm, w_uq, kv_norm, w_ukv, w_pool, pool_scale, w_out = p
    B, T, _ = h.shape
    q_lat, kv_lat, k_pe, gate_a, pool_in, gate_b = jnp.split(h @ w_in, SPLITS_AB, axis=-1)
    q = (rmsnorm(q_lat, q_norm) @ w_uq).reshape(B, T, MLA_HEADS, QK_NOPE + QK_ROPE)
    q_nope, q_pe = q[..., :QK_NOPE], q[..., QK_NOPE:]
    ckv = rmsnorm(kv_lat, kv_norm)
    if ctx_ckv is None:
        keys_ckv, keys_pe = ckv, k_pe
    else:
        rows = T // GRID_W
        q_pe = axial_rope(q_pe, rows)
        keys_ckv = jnp.concatenate([ckv, ctx_ckv.astype(ckv.dtype)], axis=1)
        keys_pe = jnp.concatenate([axial_rope(k_pe, rows), ctx_kpe.astype(k_pe.dtype)], axis=1)
    Tk = keys_ckv.shape[1]
    kv = (keys_ckv @ w_ukv).reshape(B, Tk, MLA_HEADS, QK_NOPE + V_DIM)
    k_nope, v = kv[..., :QK_NOPE], kv[..., QK_NOPE:]
    o_a = mla_attend(q_nope, q_pe, k_nope, keys_pe, v)
    o_b = multiscale_pool(pool_in, w_pool, pool_scale)
    mixed = jnp.concatenate([o_a * jax.nn.silu(gate_a), o_b * jax.nn.silu(gate_b)], axis=-1)
    return mixed @ w_out, ckv, k_pe


def mixer_c(h, p):
    w_in, ln_g, ln_b, w_s, b_s, w_out = p
    B, T, _ = h.shape
    u, v, gate = jnp.split(h @ w_in, 3, axis=-1)
    u = jax.nn.gelu(u)
    v = layernorm(jax.nn.gelu(v), ln_g, ln_b)
    vb = v.reshape(B, T // CHUNK, CHUNK, SGU_GROUPS, SGU_GROUP_DIM)
    sv = jnp.einsum('gpq,bnqgc->bnpgc', w_s, vb) + b_s.T[:, :, None]
    sv = sv.reshape(B, T, C_WIDTH)
    return (u * sv * jax.nn.silu(gate)) @ w_out


def trunk(x, cond, layer_p, even_p, odd_p, cache_ckv, cache_kpe):
    w_ada, b_ada, norm_pre, norm_post = layer_p
    ckvs, kpes = [], []
    for l in range(DEPTH):
        shift, scale, gate = jnp.split(jax.nn.silu(cond) @ w_ada[l] + b_ada[l], 3, axis=-1)
        h = rmsnorm(x, norm_pre[l]) * (1 + scale) + shift
        if l % 2 == 0:
            i = l // 2
            p = tuple(a[i] for a in even_p)
            if cache_ckv is None:
                out, ckv, kpe = mixer_ap(h, p)
                ckvs.append(ckv)
                kpes.append(kpe)
            else:
                out, _, _ = mixer_ap(h, p, cache_ckv[:, i], cache_kpe[:, i])
        else:
            out = mixer_c(h, tuple(a[l // 2] for a in odd_p))
        x = x + gate * rmsnorm(out, norm_post[l])
    return x, ckvs, kpes


def setup_inputs(seed: int = 0) -> dict:
    key = jax.random.key(seed)
    ks = jax.random.split(key, 32)
    f32 = jnp.float32

    def nrm(k, shape, s=1.0):
        return jax.random.normal(k, shape, f32) * s

    def gain(k, shape):
        return 1.0 + 0.02 * jax.random.normal(k, shape, f32)

    D = D_MODEL
    return {
        "x_prompt": nrm(ks[0], (BATCH, SEQ, D)),
        "x_sample": nrm(ks[1], (DEC_BATCH, DEC_SEQ, D)),
        "cache_ckv": nrm(ks[2], (DEC_BATCH, N_EVEN, PAST_LEN, KV_LORA)),
        "cache_kpe": nrm(ks[3], (DEC_BATCH, N_EVEN, PAST_LEN, QK_ROPE)),
        "c": nrm(ks[4], (DEC_BATCH, D)),
        "c_ctx": nrm(ks[5], (D,)),
        "w_ada": nrm(ks[6], (DEPTH, D, 3 * D), 0.5 * D ** -0.5),
        "b_ada": nrm(ks[7], (DEPTH, 3 * D), 0.01),
        "norm_pre": gain(ks[8], (DEPTH, D)),
        "norm_post": gain(ks[9], (DEPTH, D)),
        "w_in_ap": nrm(ks[10], (N_EVEN, D, IN_AB), D ** -0.5),
        "q_norm": gain(ks[11], (N_EVEN, Q_LORA)),
        "w_uq": nrm(ks[12], (N_EVEN, Q_LORA, MLA_HEADS * (QK_NOPE + QK_ROPE)), Q_LORA ** -0.5),
        "kv_norm": gain(ks[13], (N_EVEN, KV_LORA)),
        "w_ukv": nrm(ks[14], (N_EVEN, KV_LORA, MLA_HEADS * (QK_NOPE + V_DIM)), KV_LORA ** -0.5),
        "w_pool": nrm(ks[15], (N_EVEN, len(POOL_WINDOWS), POOL_GROUP, POOL_GROUP), POOL_GROUP ** -0.5),
        "pool_scale": gain(ks[16], (N_EVEN, POOL_WIDTH)),
        "w_out_ap": nrm(ks[17], (N_EVEN, MIX_WIDTH_AB, D), MIX_WIDTH_AB ** -0.5),
        "w_in_c": nrm(ks[18], (N_ODD, D, IN_C), D ** -0.5),
        "sgu_ln_g": gain(ks[19], (N_ODD, C_WIDTH)),
        "sgu_ln_b": nrm(ks[20], (N_ODD, C_WIDTH), 0.01),
        "w_s": nrm(ks[21], (N_ODD, SGU_GROUPS, CHUNK, CHUNK), CHUNK ** -0.5),
        "b_s": gain(ks[22], (N_ODD, SGU_GROUPS, CHUNK)),
        "w_out_c": nrm(ks[23], (N_ODD, C_WIDTH, D), C_WIDTH ** -0.5),
    }


def reference(x_prompt, x_sample, cache_ckv, cache_kpe, c, c_ctx, w_ada, b_ada, norm_pre, norm_post,
              w_in_ap, q_norm, w_uq, kv_norm, w_ukv, w_pool, pool_scale, w_out_ap,
              w_in_c, sgu_ln_g, sgu_ln_b, w_s, b_s, w_out_c):
    layer_p = (w_ada, b_ada, norm_pre, norm_post)
    even_p = (w_in_ap, q_norm, w_uq, kv_norm, w_ukv, w_pool, pool_scale, w_out_ap)
    odd_p = (w_in_c, sgu_ln_g, sgu_ln_b, w_s, b_s, w_out_c)
    y_prompt, ckvs, kpes = trunk(x_prompt, c_ctx[None, None, :], layer_p, even_p, odd_p, None, None)
    new_ckv = jnp.stack(ckvs, axis=1)
    new_kpe = jnp.stack(kpes, axis=1)
    y_sample, _, _ = trunk(x_sample, c[:, None, :], layer_p, even_p, odd_p, cache_ckv, cache_kpe)
    return (y_prompt, y_sample, new_ckv, new_kpe)
```

```python
import math
from contextlib import ExitStack

import numpy as np
import concourse.bass as bass
import concourse.mybir as mybir
from concourse.bass_utils import run_bass_kernel_spmd

F32 = mybir.dt.float32
BF16 = mybir.dt.bfloat16
AF = mybir.ActivationFunctionType
ALU = mybir.AluOpType

D = 1024
NCORES = 8
S_T = 4096
P_SEQ = 4
P_T = 256
PAST = 256
EPS = 1e-6
SM_SCALE = 192.0 ** -0.5
POOL_W = (2, 4, 8, 16)
C1 = math.sqrt(2.0 / math.pi)
C2 = 0.044715
import os as _os
FILL = int(_os.environ.get("KFILL", "3"))
QL0, KV0, KPE0, GA0, PI0, GB0 = 0, 256, 384, 448, 960, 1472


class Tracker:
    def __init__(self, nc, es):
        self.nc = nc
        self.es = es
        self.engs = {}
        self.last_write = {}
        self.readers = {}
        self.excl = set(["W0a", "W0b", "W1a", "W1b", "W2a", "W2b", "Tp", "Mb"])

    def add(self, name, eng, unit):
        sem = self.es.enter_context(self.nc.semaphore("s_" + name))
        self.engs[name] = dict(eng=eng, sem=sem, count=0, unit=unit, seen={})

    def op(self, ename, fn, reads=(), writes=(), issuer=None):
        E = self.engs[ename]
        iname = issuer or ename
        I = self.engs[iname]
        is_dma = issuer is not None
        deps = {}

        def add(d, raw):
            if d is None:
                return
            dn, idx = d
            if dn == ename and not is_dma:
                if ename == "pe":
                    return
            if deps.get(dn, 0) < idx:
                deps[dn] = idx

        for r in reads:
            add(self.last_write.get(r), True)
            if r in self.excl:
                for dn, idx in self.readers.get(r, {}).items():
                    if dn != ename:
                        add((dn, idx), False)
        for w in writes:
            add(self.last_write.get(w), False)
            for dn, idx in self.readers.get(w, {}).items():
                add((dn, idx), False)
        for dn, idx in deps.items():
            De = self.engs[dn]
            if De["unit"] == 16:
                idx = De["count"]
            if De["unit"] == 16:
                De["cwait"] = True
            if I["seen"].get(dn, 0) < idx:
                I["eng"].wait_ge(De["sem"], idx * De["unit"])
                I["seen"][dn] = idx
        if is_dma and E.get("cwait"):
            if I["seen"].get(ename, 0) < E["count"]:
                I["eng"].wait_ge(E["sem"], E["count"] * 16)
                I["seen"][ename] = E["count"]
            E["cwait"] = False
        ins = fn()
        E["count"] += 1
        ins.then_inc(E["sem"], E["unit"])
        me = (ename, E["count"])
        for r in reads:
            rd = self.readers.setdefault(r, {})
            rd[ename] = E["count"]
        for w in writes:
            self.last_write[w] = me
            self.readers[w] = {}
        return me

    def wait_all(self, iname):
        I = self.engs[iname]
        for dn, De in self.engs.items():
            if De["count"] > 0 and dn != iname:
                I["eng"].wait_ge(De["sem"], De["count"] * De["unit"])
                I["seen"][dn] = De["count"]


class _Stop(Exception):
    pass


def build_program(S_T=S_T, P_SEQ=P_SEQ, stop=0):
    nc = bass.Bass("TRN2", target_bir_lowering=False)

    def din(name, shape):
        return nc.dram_tensor(name, list(shape), F32, kind="ExternalInput").ap()

    def dout(name, shape):
        return nc.dram_tensor(name, list(shape), F32, kind="ExternalOutput").ap()

    xs = din("xs", [S_T, D])
    xp = din("xp", [P_SEQ * P_T, D])
    cckv = din("cckv", [PAST, 128])
    ckpe = din("ckpe", [PAST, 64])
    condT = din("condT", [128, 8, 2])
    w_ada = din("w_ada", [2, D, 3 * D])
    b_adaT = din("b_adaT", [128, 2, 3, 8])
    npre_c = din("npre_c", [128, 2, 8])
    npost_c = din("npost_c", [128, 2, 8])
    w_in_ap = din("w_in_ap", [D, 1984])
    qnorm_c = din("qnorm_c", [128, 2])
    w_uq = din("w_uq", [256, 768])
    kvn = din("kvn", [1, 128])
    w_ukT = din("w_ukT", [128, 4, 128])
    w_uv = din("w_uv", [128, 4, 128])
    w_pool = din("w_pool", [128, 4, 128])
    pscale = din("pscale", [1, 512])
    w_out_ap = din("w_out_ap", [D, D])
    w_in_c = din("w_in_c", [D, 3 * D])
    ln_g = din("ln_g", [1, D])
    ln_b = din("ln_b", [1, D])
    w_sT = din("w_sT", [128, 4, 128])
    b_s_c = din("b_s_c", [128, 4])
    w_out_c = din("w_out_c", [D, D])
    ident_d = din("ident", [128, 128])
    bands_d = din("bands", [128, 20, 128])
    rope_d = din("rope_tab", [S_T, 128])

    ys = dout("ys", [S_T, D])
    yp = dout("yp", [P_SEQ * P_T, D])
    nckv = dout("nckv", [P_SEQ * P_T, 128])
    nkpe = dout("nkpe", [P_SEQ * P_T, 64])

    with ExitStack() as es:
        def sb(name, shape, dt):
            return es.enter_context(nc.sbuf_tensor("sb_" + name, list(shape), dt))

        def ps(name, shape, dt):
            return es.enter_context(nc.psum_tensor("ps_" + name, list(shape), dt))

        tk = Tracker(nc, es)
        tk.add("pe", nc.tensor, 1)
        tk.add("act", nc.scalar, 1)
        tk.add("dve", nc.vector, 1)
        tk.add("pool", nc.gpsimd, 1)
        _streams = set()

        def stream(name):
            if name not in _streams:
                tk.add(name, None, 16)
                _streams.add(name)
            return name

        W_in = sb("W_in", [128, 8, 1984], BF16)
        W_uq = sb("W_uq", [128, 2, 768], BF16)
        W_ukT = sb("W_ukT", [128, 4, 128], BF16)
        W_uv = sb("W_uv", [128, 4, 128], BF16)
        W_pool = sb("W_pool", [128, 4, 128], BF16)
        W_out = sb("W_out", [128, 8, 1024], BF16)
        W_inc = sb("W_inc", [128, 8, 3072], BF16)
        W_sT = sb("W_sT", [128, 4, 128], BF16)
        W_outc = sb("W_outc", [128, 8, 1024], BF16)
        ckvT = sb("ckvT", [128, 34 * 128], BF16)
        ckv_aug = sb("ckv_aug", [128, 34, 130], BF16)
        kpeT2 = sb("kpeT2", [128, 17 * 128], BF16)
        gateP = [sb("gateP0", [128, D], F32), sb("gateP1", [128, D], F32)]
        lng_bc = sb("lng_bc", [128, D], F32)
        lnb_bc = sb("lnb_bc", [128, D], F32)
        kvn_bc = sb("kvn_bc", [128, 128], F32)
        bands = sb("bands", [128, 20, 128], BF16)
        ident = sb("ident", [128, 128], BF16)
        modc = sb("modc", [128, 2, 3, 8, 2], F32)
        Gc = sb("Gc", [128, 2, 8, 2], F32)
        gPc = sb("gPc", [128, 2, 8, 2], F32)
        badac = sb("badac", [128, 2, 3, 8], F32)
        nprec = sb("nprec", [128, 2, 8], F32)
        npostc = sb("npostc", [128, 2, 8], F32)
        qnc = sb("qnc", [128, 2], F32)
        bsc = sb("bsc", [128, 4], F32)
        scT = sb("scT", [128, 8, 2], F32)
        mhalf = sb("mhalf", [128, 1], F32)
        epsc = sb("epsc", [128, 1], F32)
        kpe_c = sb("kpe_c", [128, 2, 64], BF16)
        stat = sb("stat", [128, 64], F32)
        X = [sb("X%d" % i, [128, D], F32) for i in range(3)]
        TB = sb("TB", [128, D], BF16)
        hT = [sb("hT%d" % i, [128, 8, 256], BF16) for i in range(2)]
        mT = sb("mT", [128, 8, 128], BF16)
        q_all = sb("q_all", [128, 4, 256], BF16)
        qn = q_all[:, 2, :]
        qnT = q_all[:, 3, :].rearrange("p (c k) -> p c k", c=2)
        qnopeT = [sb("qnopeT0", [128, 128], BF16)]
        qabsT = sb("qabsT", [128, 1024], BF16)
        qpeT2 = sb("qpeT2", [128, 1024], BF16)
        u_ring = [sb("u%d" % i, [128, 512], BF16) for i in range(5)]
        PT = [sb("PT%d" % i, [128, 512], BF16) for i in range(2)]
        olT = [sb("olT%d" % i, [128, 1024], BF16) for i in range(2)]
        oln = [sb("oln0", [128, 128], BF16)]
        dT = sb("dT", [128, 512], BF16)
        ropeT = [sb("ropeT0", [128, 128], F32)]
        rtmp = sb("rtmp", [128, 64], F32)
        rtmp2 = sb("rtmp2", [128, 64], F32)
        ckv_f = [sb("ckv_f0", [128, 128], F32)]
        kpe_f = [sb("kpe_f0", [128, 64], F32)]
        kpe_bfs = [sb("kpe_bf0", [128, 128], BF16)] * 2
        bnst = sb("bnst", [128, 2, 6], F32)
        bnag = sb("bnag", [128, 2], F32)
        W0 = ps("W0", [128, D], F32)
        W1 = ps("W1", [128, D], F32)
        W2 = ps("W2", [128, D], F32)
        Tp = ps("Tp", [128, D], BF16)
        Mb = ps("Mb", [128, 512], F32)

        def pe_mm(out, lhsT, rhs, start, stop, reads, writes, skip=False):
            if skip:
                return tk.op("pe", lambda: nc.tensor.matmul(out, lhsT, rhs, start=start, stop=stop,
                                                            skip_group_check=True), reads, writes)
            return tk.op("pe", lambda: nc.tensor.matmul(out, lhsT, rhs, start=start, stop=stop),
                         reads, writes)

        def pe_tr(out, in_, reads, writes, idn=None):
            idn = ident if idn is None else idn
            return tk.op("pe", lambda: nc.tensor.transpose(out, in_, idn[:]), reads, writes)

        def act(out, in_, func, reads, writes, scale=1.0, bias=None, accum=None):
            kw = {}
            if bias is not None:
                kw["bias"] = bias
            if accum is not None:
                kw["accum_out"] = accum
            return tk.op("act", lambda: nc.scalar.activation(out=out, in_=in_, func=func, scale=scale, **kw),
                         reads, writes)

        def veng(e):
            return nc.vector if e == "dve" else nc.gpsimd

        def v_copy(e, out, in_, reads, writes):
            return tk.op(e, lambda: veng(e).tensor_copy(out=out, in_=in_), reads, writes)

        def v_ts(e, out, in0, s1, s2, op0, op1, reads, writes):
            if s2 is None:
                return tk.op(e, lambda: veng(e).tensor_scalar(out=out, in0=in0, scalar1=s1, scalar2=None, op0=op0),
                             reads, writes)
            return tk.op(e, lambda: veng(e).tensor_scalar(out=out, in0=in0, scalar1=s1, scalar2=s2, op0=op0, op1=op1),
                         reads, writes)

        def v_tt(e, out, in0, in1, op, reads, writes):
            return tk.op(e, lambda: veng(e).tensor_tensor(out=out, in0=in0, in1=in1, op=op), reads, writes)

        def v_stt(e, out, in0, scalar, in1, op0, op1, reads, writes):
            return tk.op(e, lambda: veng(e).scalar_tensor_tensor(out=out, in0=in0, scalar=scalar, in1=in1,
                                                                  op0=op0, op1=op1), reads, writes)

        def ckpt(n):
            if stop == n:
                raise _Stop()

        def dma(strm, issuer, out, in_, reads, writes):
            q = nc.sync if issuer == "sp" else nc.gpsimd
            return tk.op(stream(strm), lambda: q.dma_start(out=out, in_=in_), reads, writes,
                         issuer=issuer)

        tk.add("sp", nc.sync, 1)

        _stat_i = [0]

        def statcol():
            i = _stat_i[0] % 60
            _stat_i[0] += 1
            return stat[:, i:i + 1], ("stat", i)

        rstd_mode = ["pool"]

        def rstd_from(ms_ap, ms_key, eps):
            if rstd_mode[0] == "act":
                t_ap, t_key = statcol()
                act(t_ap, ms_ap, AF.Ln, [ms_key, "epsc"], [t_key], bias=epsc[:])
                r_ap, r_key = statcol()
                act(r_ap, t_ap, AF.Exp, [t_key], [r_key], scale=-0.5)
                return r_ap, r_key
            t_ap, t_key = statcol()
            v_ts("pool", t_ap, ms_ap, eps, None, ALU.add, None, [ms_key], [t_key])
            r_ap, r_key = statcol()
            v_tt("pool", r_ap, t_ap, mhalf[:], ALU.pow, [t_key, "mhalf"], [r_key])
            return r_ap, r_key

        tk.op("pool", lambda: nc.gpsimd.memset(mhalf[:], -0.5), [], ["mhalf"])
        tk.op("pool", lambda: nc.gpsimd.memset(epsc[:], EPS), [], ["epsc"])
        tk.op("pool", lambda: nc.gpsimd.memset(kpe_bfs[0][:], 0.0), [], ["kpe_bf"])
        tk.op("dve", lambda: nc.vector.memset(ckv_aug[:, :, 128:130], 1.0), [], ["ckv_aug_ones"])

        dma("c_small", "sp", scT[:], condT[:], [], ["scT"])
        dma("c_small", "sp", badac[:], b_adaT[:], [], ["badac"])
        dma("c_small", "sp", nprec[:], npre_c[:], [], ["nprec"])
        dma("c_small", "sp", npostc[:], npost_c[:], [], ["npostc"])
        dma("c_small", "sp", qnc[:], qnorm_c[:], [], ["qnc"])
        dma("c_small", "sp", bsc[:], b_s_c[:], [], ["bsc"])
        dma("c_small", "sp", kvn_bc[:], kvn.partition_broadcast(128), [], ["kvn_bc"])
        dma("c_small", "sp", lng_bc[:], ln_g.partition_broadcast(128), [], ["lng_bc"])
        dma("c_small", "sp", lnb_bc[:], ln_b.partition_broadcast(128), [], ["lnb_bc"])
        dma("c_cast", "pool", ident[:], ident_d[:], [], ["ident"])
        dma("c_cast", "pool", bands[:], bands_d[:], [], ["bands"])

        act(stat[:, 48:64], scT[:].rearrange("p k j -> p (k j)"), AF.Tanh, ["scT"], ["sc_th"], scale=0.5)
        v_stt("dve", scT[:].rearrange("p k j -> p (k j)"), stat[:, 48:64], 1.0,
              scT[:].rearrange("p k j -> p (k j)"), ALU.add, ALU.mult, ["sc_th", "scT"], ["scT"])
        v_ts("dve", scT[:].rearrange("p k j -> p (k j)"), scT[:].rearrange("p k j -> p (k j)"), 0.5, None,
             ALU.mult, None, ["scT"], ["scT"])
        _stat_i[0] = 0

        def gen_ada(l, stages):
            ns = len(stages)
            cnt = 0
            for k in range(8):
                for part in range(3):
                    W_ = stages[0][1]
                    for sub in range(D // W_):
                        st_ap, _, strm, keys = stages[cnt % ns]
                        cnt += 1
                        c0 = part * D + sub * W_
                        dma(strm, "sp", st_ap, w_ada[l, k * 128:(k + 1) * 128, c0:c0 + W_], [], keys)
                        for c_ in range(W_ // 128):
                            cc = sub * (W_ // 128) + c_
                            col = ((l * 3 + part) * 8 + cc) * 2
                            pe_mm(Mb[:, col:col + 2], st_ap[:, c_ * 128:(c_ + 1) * 128], scT[:, k, :],
                                  (k == 0 and part == 0 and cc == 0), k == 7,
                                  keys + ["scT"], ["Mb"], skip=True)
                        yield
            lo = l * 48
            v_copy("dve", modc[:, l].rearrange("p a c j -> p (a c j)"), Mb[:, lo:lo + 48], ["Mb"], [("modc", l)])
            for j in range(2):
                v_tt("dve", modc[:, l, :, :, j], modc[:, l, :, :, j], badac[:, l], ALU.add,
                     [("modc", l), "badac"], [("modc", l)])
            for j in range(2):
                v_stt("dve", Gc[:, l, :, j], modc[:, l, 1, :, j], 1.0, nprec[:, l, :], ALU.add, ALU.mult,
                      [("modc", l), "nprec"], [("Gc", l)])
                v_tt("dve", gPc[:, l, :, j], modc[:, l, 2, :, j], npostc[:, l, :], ALU.mult,
                     [("modc", l), "npostc"], [("gPc", l)])
            yield

        def build_gateP(j, l):
            tk.op("dve", lambda: nc.vector.memset(X[2][:, 0:128], 1.0), [], ["X2"])
            for cc in range(8):
                xb = cc % 2
                v_ts("dve", X[xb][:, 0:128], ident[:], gPc[:, l, cc, j:j + 1], None, ALU.mult, None,
                     ["ident", ("gPc", l)], ["X%d" % xb])
                pe_mm(W2[:, cc * 128:(cc + 1) * 128], X[2][:, 0:128], X[xb][:, 0:128], True, True,
                      ["X2", "X%d" % xb], ["W2" + "ab"[cc // 4]])
            act(gateP[l][:], W2[:], AF.Identity, ["W2a", "W2b"], ["gateP%d" % l])

        for _ in gen_ada(0, [(X[i][:], D, "ldX%d" % i, ["X%d" % i]) for i in range(3)]):
            pass

        w_in_v = w_in_ap.rearrange("(k p) n -> p k n", p=128)
        dma("w_inkv", "pool", W_in[:, :, KV0:KV0 + 192], w_in_v[:, :, KV0:KV0 + 192], [], ["W_inkv"])
        if S_T // 128 + 2 <= 34:
            for c in range(2):
                jk = S_T // 128 + c
                dma("c_cache", "pool", ckv_aug[:, jk, 0:128], cckv[c * 128:(c + 1) * 128, :], [], [("ckv_aug", jk)])
                dma("c_cache", "pool", kpe_c[:, c, :], ckpe[c * 128:(c + 1) * 128, :], [], [("kpe_c", c)])
        for k in range(8):
            dma("w_in", "pool", W_in[:, k, 0:KV0], w_in_v[:, k, 0:KV0], [], ["W_in"])
            dma("w_in", "pool", W_in[:, k, KV0 + 192:1984], w_in_v[:, k, KV0 + 192:1984], [], ["W_in"])
        dma("w_small", "pool", W_uq[:], w_uq.rearrange("(k p) n -> p k n", p=128), [], ["W_uq"])
        dma("w_small", "pool", W_ukT[:], w_ukT[:], [], ["W_ukT"])
        dma("w_small", "pool", W_uv[:], w_uv[:], [], ["W_uv"])
        dma("w_small", "pool", W_sT[:], w_sT[:], [], ["W_sT"])
        dma("ldX0", "sp", X[0][:, 0:512], w_pool.rearrange("p g c -> p (g c)"), [], ["X0"])
        dma("ldX1", "sp", X[1][:, 0:512], pscale.partition_broadcast(128), [], ["X1"])
        v_tt("dve", W_pool[:].rearrange("p g c -> p (g c)"), X[0][:, 0:512], X[1][:, 0:512], ALU.mult,
             ["X0", "X1"], ["W_pool"])
        w_out_v = w_out_ap.rearrange("(k p) n -> p k n", p=128)
        for k in range(8):
            dma("w_out", "pool", W_out[:, k, :], w_out_v[:, k, :], [], ["W_out"])
        w_inc_v = w_in_c.rearrange("(k p) n -> p k n", p=128)
        for k in range(8):
            dma("w_inc", "pool", W_inc[:, k, :], w_inc_v[:, k, :], [], ["W_inc"])
        w_outc_v = w_out_c.rearrange("(k p) n -> p k n", p=128)
        for k in range(8):
            dma("w_outc", "pool", W_outc[:, k, :], w_outc_v[:, k, :], [], ["W_outc"])

        build_gateP(0, 0)
        xrot = [0]
        tslot = [0]
        evac_flip = [0]
        tbrot = [0]
        kvrot = [0]
        TBs = [(TB, "TB"), (TB, "TB")]

        def tslot_next():
            s = tslot[0] % 8
            tslot[0] += 1
            return s

        def klist(k):
            return list(k) if isinstance(k, list) else [k]

        def norm_to_T(src_ap, src_keys, width_scale, tb_ap, tb_key, junk=None):
            ms_ap, ms_key = statcol()
            junk_ap, junk_keys = (tb_ap, klist(tb_key)) if junk is None else junk
            tk.op("dve", lambda: nc.vector.scalar_tensor_tensor(
                out=junk_ap[:], in0=src_ap, scalar=width_scale * width_scale, in1=src_ap,
                op0=ALU.mult, op1=ALU.mult, accum_out=ms_ap), src_keys, junk_keys + [ms_key])
            r_ap, r_key = rstd_from(ms_ap, ms_key, EPS)
            v_ts("dve", tb_ap[:], src_ap, r_ap, None, ALU.mult, None, src_keys + [r_key], klist(tb_key))

        def T_to_hT(dest, dest_key, l, j, tb_ap, tb_key):
            for k in range(8):
                pe_tr(Tp[:, k * 128:(k + 1) * 128], tb_ap[:, k * 128:(k + 1) * 128], klist(tb_key) + ["ident"], ["Tp"])
            evac_flip[0] += 1
            for k in range(8):
                g_ap = Gc[:, l, k, j:j + 1]
                s_ap = modc[:, l, 0, k, j:j + 1]
                if True:
                    v_ts("dve", dest(k), Tp[:, k * 128:(k + 1) * 128], g_ap, s_ap, ALU.mult, ALU.add,
                         ["Tp", ("Gc", l), ("modc", l)], [(dest_key, k)])
                else:
                    act(dest(k), Tp[:, k * 128:(k + 1) * 128], AF.Identity, ["Tp", ("Gc", l), ("modc", l)],
                        [(dest_key, k)], scale=g_ap, bias=s_ap)

        def evac_mT():
            evac_flip[0] += 1
            src = Tp[:].rearrange("p (k c) -> p k c", k=8)
            if False:
                v_copy("dve", mT[:], src, ["Tp"], [("mT", k) for k in range(8)])
            else:
                act(mT[:], src, AF.Identity, ["Tp"], [("mT", k) for k in range(8)])

        def gen_make_hT(x_rows, dest, dest_key, j, xb=None, tbsel=None, junk=None):
            if xb is None:
                xb = xrot[0] % 3
                xrot[0] += 1
            if tbsel is None:
                tb_ap, tb_key = TBs[tbrot[0] % 2]
                tbrot[0] += 1
            else:
                tb_ap, tb_key = tbsel
            dma("ldX%d" % xb, "sp", X[xb][:], x_rows, [], ["X%d" % xb])
            norm_to_T(X[xb][:], ["X%d" % xb], 1.0 / 32.0, tb_ap, tb_key, junk=junk)
            T_to_hT(dest, dest_key, 0, j, tb_ap, tb_key)
            yield

        def rope_apply(src, src_keys, dst, dst_keys, tab, tab_key):
            s3 = src.rearrange("p (a f) -> p a f", a=2)
            t3 = rtmp[:].rearrange("p (a f) -> p a f", a=2)
            S3 = tab[:, 64:128].rearrange("p (a f) -> p a f", a=2)
            v_tt("dve", t3[:, :, 0:16], s3[:, :, 16:32], S3[:, :, 0:16], ALU.mult, src_keys + [tab_key], ["rtmp_a"])
            v_tt("dve", t3[:, :, 16:32], s3[:, :, 0:16], S3[:, :, 16:32], ALU.mult, src_keys + [tab_key], ["rtmp_b"])
            v_tt("dve", rtmp2[:], src, tab[:, 0:64], ALU.mult, src_keys + [tab_key], ["rtmp2"])
            v_tt("dve", dst, rtmp2[:], rtmp[:], ALU.add, ["rtmp2", "rtmp_a", "rtmp_b"], dst_keys)

        def hT_mm(out, out_keys, lhs_of_k, lhs_keys_of_k, Wt, wkey, c0, c1):
            for k in range(8):
                pe_mm(out, lhs_of_k(k), Wt[:, k, c0:c1], k == 0, k == 7, [lhs_keys_of_k(k), wkey], out_keys)

        ATT_BANKS = [(W0, 0, "W0a"), (W0, 512, "W0b"), (W1, 0, "W1a"), (W1, 512, "W1b")]

        class Seq:
            pass

        def make_seq(x_dram, y_dram, row0, ntiles, j, is_sample, p_row0=0, gb0=0):
            q = Seq()
            q.x, q.y, q.row0, q.ntiles, q.j, q.is_sample, q.p_row0 = x_dram, y_dram, row0, ntiles, j, is_sample, p_row0
            q.nk = ntiles + (2 if is_sample else 0)
            q.last = ntiles - 1
            q.u_done = set()
            q.prepA_done = set()
            q.u_off = 0
            q.mk_opts = {}
            q.p1_opts = lambda i: {}
            q.gb0 = gb0
            if is_sample:
                q.p1_dest = lambda i: ((i % 4) // 2, i % 2)
            else:
                q.p1_dest = lambda i: (gb0 % 2, i % 2)
            return q

        def xrows(q, i):
            return q.x[q.row0 + i * 128: q.row0 + (i + 1) * 128, :]

        def load_rope(i):
            rb = 0
            dma("ldR%d" % rb, "sp", ropeT[rb][:], rope_d[i * 128:(i + 1) * 128, :], [], ["ropeT%d" % rb])
            return ropeT[rb], "ropeT%d" % rb

        def put_key_tile(jk):
            s = tslot_next()
            pe_tr(Tp[:, s * 128:(s + 1) * 128], ckv_aug[:, jk, 0:128], [("ckv_aug", jk), "ident"], ["Tp"])
            v_copy("dve", ckvT[:, jk * 128:(jk + 1) * 128], Tp[:, s * 128:(s + 1) * 128], ["Tp"],
                   [("ckvT", jk)])
            s = tslot_next()
            hf = jk % 2
            pe_tr(Tp[:, s * 128:(s + 1) * 128], kpe_bfs[hf][:], ["kpe_bf", "ident"], ["Tp"])
            v_copy("dve", kpeT2[hf * 64:(hf + 1) * 64, (jk // 2) * 128:(jk // 2 + 1) * 128],
                   Tp[hf * 64:(hf + 1) * 64, s * 128:(s + 1) * 128], ["Tp"], [("kpeT2", jk)])

        def gen_phase1_tile(q, i):
            hbuf, hreg = q.p1_dest(i)
            dest = lambda k: hT[hbuf][:, k, hreg * 128:(hreg + 1) * 128]
            dkey = ("hT", hbuf, hreg)
            yield from gen_make_hT(xrows(q, i), dest, dkey, q.j, **q.p1_opts(i))
            KVt, kc, kkey = ATT_BANKS[kvrot[0] % 4]
            kvrot[0] += 1
            KV = KVt[:, kc:kc + 192]
            hT_mm(KV, [kkey], dest, lambda k: (dkey, k), W_in, "W_inkv", KV0, KV0 + 192)
            ms_ap, ms_key = statcol()
            act(qn[:, 0:128], KV[:, 0:128], AF.Square, [kkey], ["qn", ms_key],
                scale=128.0 ** -0.5, accum=ms_ap)
            r_ap, r_key = rstd_from(ms_ap, ms_key, EPS)
            yield
            cb = 0
            v_stt("dve", ckv_f[cb][:], KV[:, 0:128], r_ap, kvn_bc[:], ALU.mult, ALU.mult,
                  [kkey, r_key, "kvn_bc"], ["ckv_f%d" % cb])
            hf = i % 2
            if q.is_sample:
                tab, tabk = load_rope(i)
                rope_apply(KV[:, 128:192], [kkey], kpe_bfs[hf][:, hf * 64:(hf + 1) * 64], ["kpe_bf"], tab, tabk)
                act(ckv_aug[:, i, 0:128], ckv_f[cb][:], AF.Identity, ["ckv_f%d" % cb], [("ckv_aug", i)])
            else:
                v_copy("dve", kpe_f[cb][:], KV[:, 128:192], [kkey], ["kpe_f%d" % cb])
                act(ckv_aug[:, i, 0:128], ckv_f[cb][:], AF.Identity, ["ckv_f%d" % cb], [("ckv_aug", i)])
                act(kpe_bfs[hf][:, hf * 64:(hf + 1) * 64], kpe_f[cb][:], AF.Identity, ["kpe_f%d" % cb],
                    ["kpe_bf"])
                r0 = q.p_row0 + i * 128
                dma("st_ckv%d" % cb, "pool", nckv[r0:r0 + 128, :], ckv_f[cb][:], ["ckv_f%d" % cb], [])
                dma("st_kpe%d" % cb, "pool", nkpe[r0:r0 + 128, :], kpe_f[cb][:], ["kpe_f%d" % cb], [])
            put_key_tile(i)
            yield

        def gen_cache_tiles(q):
            for c in range(2):
                jk = q.ntiles + c
                hf = jk % 2
                v_copy("dve", kpe_bfs[hf][:, hf * 64:(hf + 1) * 64], kpe_c[:, c, :], [("kpe_c", c)], ["kpe_bf"])
                put_key_tile(jk)
                yield

        def prepA_tile(q, b, tb, xb=None, u_bank=None):
            i = 2 * b + tb
            hb = (q.gb0 + b) % 2
            lhs = lambda k: hT[hb][:, k, tb * 128:(tb + 1) * 128]
            dkey = ("hT", hb, tb)
            lkeys = lambda k: (dkey, k)
            opts = dict(q.mk_opts)
            if xb is not None:
                opts["xb"] = xb
            for _ in gen_make_hT(xrows(q, i), lhs, dkey, q.j, **opts):
                pass
            if i not in q.u_done:
                if u_bank is None:
                    U, ukey = W0[:, 0:512], "W0a"
                else:
                    U, ukey = u_bank
                hT_mm(U, [ukey], lhs, lkeys, W_in, "W_in", PI0, PI0 + 512)
                ui = (i + q.u_off) % 5
                act(u_ring[ui][:], U, AF.Identity, [ukey], [("u", ui)])
                q.u_done.add(i)
            q.prepA_done.add((b, tb))

        def gen_prepQ_tile(q, b, tb):
            i = 2 * b + tb
            hb = (q.gb0 + b) % 2
            lhs = lambda k: hT[hb][:, k, tb * 128:(tb + 1) * 128]
            dkey = ("hT", hb, tb)
            lkeys = lambda k: (dkey, k)
            while (b, tb) not in q.prepA_done:
                yield
            QL = W0[:, 512:768]
            hT_mm(QL, ["W0b"], lhs, lkeys, W_in, "W_in", QL0, QL0 + 256)
            ms_ap, ms_key = statcol()
            act(qn[:], QL, AF.Square, ["W0b"], ["qn", ms_key], scale=1.0 / 16.0, accum=ms_ap)
            r_ap, r_key = rstd_from(ms_ap, ms_key, EPS)
            act(qn[:], QL, AF.Identity, ["W0b", r_key], ["qn"], scale=r_ap)
            yield
            for c in range(2):
                pe_tr(Tp[:, c * 128:(c + 1) * 128], qn[:, c * 128:(c + 1) * 128], ["qn", "ident"], ["Tp"])
            v_tt("dve", qnT, Tp[:, 0:256].rearrange("p (c k) -> p c k", c=2),
                 qnc[:, 0:2].unsqueeze(2).to_broadcast([128, 2, 128]), ALU.mult,
                 ["Tp", "qnc"], [("qnT", 0), ("qnT", 1)])
            for c in range(2):
                pe_mm(W1[:, 0:512], qnT[:, c, :], W_uq[:, c, 0:512], c == 0, c == 1,
                      [("qnT", c), "W_uq"], ["W1a"])
            for c in range(2):
                pe_mm(W1[:, 512:768], qnT[:, c, :], W_uq[:, c, 512:768], c == 0, c == 1,
                      [("qnT", c), "W_uq"], ["W1b"])
            if q.is_sample:
                tab, tabk = load_rope(i)
            QK = ["W1a", "W1b"]
            qv = W1[:, 0:768].rearrange("p (h c) -> p h c", h=4)
            v_copy("dve", q_all[:, :, 0:128], qv[:, :, 0:128], QK, ["q_all", "qn", ("qnT", 0), ("qnT", 1)])
            src = qv[:, :, 128:192]
            if q.is_sample:
                tA = PT[0][:].bitcast(F32).rearrange("p (h c) -> p h c", h=4)
                tB = PT[1][:].bitcast(F32).rearrange("p (h c) -> p h c", h=4)
                s4 = src.rearrange("p h (a f) -> p h a f", a=2)
                t4 = tA.rearrange("p h (a f) -> p h a f", a=2)
                Sb = tab[:, 64:128].rearrange("p (a f) -> p a f", a=2).unsqueeze(1).to_broadcast([128, 4, 2, 32])
                Cb = tab[:, 0:64].unsqueeze(1).to_broadcast([128, 4, 64])
                v_tt("dve", t4[:, :, :, 0:16], s4[:, :, :, 16:32], Sb[:, :, :, 0:16], ALU.mult, QK + [tabk], ["PT0"])
                v_tt("dve", t4[:, :, :, 16:32], s4[:, :, :, 0:16], Sb[:, :, :, 16:32], ALU.mult, QK + [tabk], ["PT0"])
                v_tt("dve", tB, src, Cb, ALU.mult, QK + [tabk], ["PT1"])
                v_tt("dve", q_all[:, :, 128:192], tB, tA, ALU.add, ["PT0", "PT1"], ["q_all"])
            else:
                v_copy("dve", q_all[:, :, 128:192], src, QK, ["q_all"])
            v_copy("pool", q_all[:, :, 192:256], q_all[:, :, 128:192], ["q_all"], ["q_all"])
            yield
            for h in range(4):
                pe_tr(Tp[:, h * 128:(h + 1) * 128], q_all[:, h, 0:128], ["q_all", "ident"], ["Tp"])
            for h in range(4):
                pe_tr(Tp[:, (4 + h) * 128:(5 + h) * 128], q_all[:, h, 128:256], ["q_all", "ident"], ["Tp"])
            qnT_all = PT[0][:]
            v_copy("dve", qnT_all, Tp[:, 0:512], ["Tp"], ["PT0"])
            qpe_dst = qpeT2[:].rearrange("p (h t c) -> p h t c", h=4, t=2)[:, :, tb, :]
            v_copy("dve", qpe_dst, Tp[:, 512:1024].rearrange("p (h c) -> p h c", h=4), ["Tp"],
                   [("qpeT2", h, tb) for h in range(4)])
            for h in range(4):
                pe_mm(W1[:, h * 128:(h + 1) * 128], W_ukT[:, h, :], qnT_all[:, h * 128:(h + 1) * 128], True, True,
                      ["W_ukT", "PT0"], ["W1a"])
            qab_dst = qabsT[:].rearrange("p (h t c) -> p h t c", h=4, t=2)[:, :, tb, :]
            act(qab_dst, W1[:, 0:512].rearrange("p (h c) -> p h c", h=4), AF.Identity, ["W1a"],
                [("qabsT", h, tb) for h in range(4)])
            yield

        def gen_prepQ(q, b):
            yield from gen_prepQ_tile(q, b, 0)
            yield from gen_prepQ_tile(q, b, 1)

        def gen_prep(q, b):
            prepA_tile(q, b, 0)
            prepA_tile(q, b, 1)
            yield
            yield from gen_prepQ(q, b)

        def gen_attn(q, b):
            nk = q.nk
            gpar = (q.gb0 + b) % 2
            ol = olT[gpar]
            for hp in range(2):
                qa = qabsT[:, hp * 512:(hp + 1) * 512]
                qa_keys = [("qabsT", 2 * hp + hl, tb) for hl in range(2) for tb in range(2)]
                qp_keys = [("qpeT2", 2 * hp + hl, tb) for hl in range(2) for tb in range(2)]
                def emit_S(jk):
                    sbk = jk % 2
                    S = W0[:, sbk * 512:(sbk + 1) * 512]
                    skey = "W0" + "ab"[sbk]
                    hf = jk % 2
                    pe_mm(S, ckvT[:, jk * 128:(jk + 1) * 128], qa, True, False, [("ckvT", jk)] + qa_keys, [skey])
                    pe_mm(S, kpeT2[hf * 64:(hf + 1) * 64, (jk // 2) * 128:(jk // 2 + 1) * 128],
                          qpeT2[hf * 64:(hf + 1) * 64, hp * 512:(hp + 1) * 512], False, True,
                          [("kpeT2", jk)] + qp_keys, [skey])
                    act(PT[sbk][:], S, AF.Exp, [skey], ["PT%d" % sbk], scale=SM_SCALE)

                def emit_PV(jk):
                    sbk = jk % 2
                    for hl in range(2):
                        for tb in range(2):
                            oc = tb * 256
                            pe_mm(W1[:, hl * 512 + oc: hl * 512 + oc + 129],
                                  PT[sbk][:, hl * 256 + tb * 128: hl * 256 + (tb + 1) * 128],
                                  ckv_aug[:, jk, 0:129], (jk == 0 and tb == 0), jk == nk - 1,
                                  ["PT%d" % sbk, ("ckv_aug", jk), "ckv_aug_ones"], ["W1" + "ab"[hl]],
                                  skip=True)

                emit_S(0)
                if nk > 1:
                    emit_S(1)
                for jk in range(nk):
                    emit_PV(jk)
                    if jk + 2 < nk:
                        emit_S(jk + 2)
                    yield
                rl_ap = stat[:, 60:64]
                ov = W1[:].rearrange("p (a c) -> p a c", a=4)
                tk.op("dve", lambda: nc.vector.reciprocal(out=rl_ap.unsqueeze(2), in_=ov[:, :, 128:129]),
                      ["W1a", "W1b"], ["rl4"])
                on_all = q_all[:, 0:2, :].rearrange("p a (b c) -> p (a b) c", b=2)
                v_tt("dve", on_all, ov[:, :, 0:128], rl_ap.unsqueeze(2).to_broadcast([128, 4, 128]), ALU.mult,
                     ["W1a", "W1b", "rl4"], ["q_all"])
                for a4 in range(4):
                    pe_tr(Tp[:, a4 * 128:(a4 + 1) * 128], on_all[:, a4, :], ["q_all", "ident"], ["Tp"])
                v_copy("dve", ol[:, hp * 512:(hp + 1) * 512], Tp[:, 0:512], ["Tp"],
                       [("olT", gpar, 2 * hp + hl, tb) for hl in range(2) for tb in range(2)])
                yield

        def gelu2(wkeys, Xf, kx):
            act(Xf[:], W2[:], AF.Square, wkeys, [kx], scale=math.sqrt(C2))
            yield
            v_stt("dve", Xf[:], Xf[:], 1.0, W2[:], ALU.add, ALU.mult, [kx] + wkeys, [kx])
            yield
            act(Xf[:], Xf[:], AF.Tanh, [kx], [kx], scale=C1)
            yield
            v_stt("dve", Xf[:], Xf[:], 1.0, W2[:], ALU.add, ALU.mult, [kx] + wkeys, [kx])
            yield

        def gen_finish_tile(q, b, tb, embed=None):
            i = 2 * b + tb
            j = q.j
            hb = (q.gb0 + b) % 2
            ol = olT[hb]
            lhs = lambda k: hT[hb][:, k, tb * 128:(tb + 1) * 128]
            lkeys = lambda k: (("hT", hb, tb), k)
            xa = xrot[0] % 3
            xf0 = (xrot[0] + 1) % 3
            xf1 = (xrot[0] + 2) % 3
            xrot[0] += 1
            Xa, Xf0, Xf1 = X[xa], X[xf0], X[xf1]
            ka, k0, k1 = "X%d" % xa, "X%d" % xf0, "X%d" % xf1
            WK = ["W2a", "W2b"]
            dma("ldX%d" % xa, "sp", Xa[:], xrows(q, i), [], [ka])
            hT_mm(W2[:, 0:512], ["W2a"], lhs, lkeys, W_in, "W_in", GA0, GA0 + 512)
            hT_mm(W2[:, 512:1024], ["W2b"], lhs, lkeys, W_in, "W_in", GB0, GB0 + 512)
            var = 0 if i == 0 else (2 if i == q.last else 1)
            for g in range(4):
                srcs = []
                if i > 0:
                    srcs.append((i - 1, g * 5 + 3))
                srcs.append((i, g * 5 + var))
                if i < q.last:
                    srcs.append((i + 1, g * 5 + 4))
                for n_, (si, bi) in enumerate(srcs):
                    ui = (si + q.u_off) % 5
                    pe_mm(Mb[:, g * 128:(g + 1) * 128], u_ring[ui][:, g * 128:(g + 1) * 128],
                          bands[:, bi, :], n_ == 0, n_ == len(srcs) - 1,
                          [("u", ui), "bands"], ["Mb"])
            v_copy("dve", dT[:], Mb[:, 0:512], ["Mb"], ["dT"])
            yield
            if embed is not None:
                embed(tb, xf1)
            act(Xf0[:], W2[:], AF.Tanh, WK, [k0], scale=0.5)
            yield
            v_stt("dve", Xf0[:], Xf0[:], 1.0, W2[:], ALU.add, ALU.mult, [k0] + WK, [k0])
            yield
            for h in range(4):
                col = h * 256 + tb * 128
                pe_mm(W2[:, h * 128:(h + 1) * 128], ol[:, col:col + 128], W_uv[:, h, :], True, True,
                      [("olT", hb, h, tb), "W_uv"], ["W2a"])
            for g in range(4):
                pe_mm(W2[:, 512 + g * 128: 512 + (g + 1) * 128], dT[:, g * 128:(g + 1) * 128],
                      W_pool[:, g, :], True, True, ["dT", "W_pool"], ["W2b"])
            v_stt("dve", TB[:], W2[:], 0.5, Xf0[:], ALU.mult, ALU.mult, WK + [k0], ["TB"])
            yield
            for k in range(8):
                pe_tr(Tp[:, k * 128:(k + 1) * 128], TB[:, k * 128:(k + 1) * 128], ["TB", "ident"], ["Tp"])
            evac_mT()
            yield
            for nb in range(2):
                for k in range(8):
                    pe_mm(W2[:, nb * 512:(nb + 1) * 512], mT[:, k, :], W_out[:, k, nb * 512:(nb + 1) * 512],
                          k == 0, k == 7, [("mT", k), "W_out"], ["W2" + "ab"[nb]])
            ms_ap, ms_key = statcol()
            act(TB[:], W2[:], AF.Square, WK, ["TB", ms_key], scale=1.0 / 32.0, accum=ms_ap)
            yield
            r_ap, r_key = rstd_from(ms_ap, ms_key, EPS)
            yield
            v_stt("dve", Xf1[:], W2[:], r_ap, gateP[0][:], ALU.mult, ALU.mult, WK + [r_key, "gateP0"], [k1])
            yield
            v_tt("dve", Xa[:], Xf1[:], Xa[:], ALU.add, [k1, ka], [ka])
            yield
            norm_to_T(Xa[:], [ka], 1.0 / 32.0, TB, "TB")
            yield
            T_to_hT(lambda k: mT[:, k, :], "mT", 1, j, TB, "TB")
            yield
            mk = lambda k: ("mT", k)
            yield
            for nb in range(2):
                hT_mm(W2[:, nb * 512:(nb + 1) * 512], ["W2" + "ab"[nb]], lambda k: mT[:, k, :], mk, W_inc,
                      "W_inc", nb * 512, (nb + 1) * 512)
            yield from gelu2(WK, Xf0, k0)
            yield
            for nb in range(2):
                hT_mm(W2[:, nb * 512:(nb + 1) * 512], ["W2" + "ab"[nb]], lambda k: mT[:, k, :], mk, W_inc,
                      "W_inc", 1024 + nb * 512, 1024 + (nb + 1) * 512)
            yield from gelu2(WK, Xf1, k1)
            yield
            tk.op("dve", lambda: nc.vector.bn_stats(bnst[:, 0, :], Xf1[:, 0:512]), [k1], ["bnst0"])
            yield
            tk.op("dve", lambda: nc.vector.bn_stats(bnst[:, 1, :], Xf1[:, 512:1024]), [k1], ["bnst1"])
            yield
            tk.op("dve", lambda: nc.vector.bn_aggr(bnag[:], bnst[:]), ["bnst0", "bnst1"], ["bnag"])
            yield
            t_ap, t_key = statcol()
            v_ts("pool", t_ap, bnag[:, 1:2], 4.0 * EPS, None, ALU.add, None, ["bnag"], [t_key])
            yield
            r2_ap, r2_key = statcol()
            v_tt("pool", r2_ap, t_ap, mhalf[:], ALU.pow, [t_key, "mhalf"], [r2_key])
            yield
            v_stt("dve", Xf1[:], Xf1[:], bnag[:, 0:1], lng_bc[:], ALU.subtract, ALU.mult, [k1, "bnag", "lng_bc"], [k1])
            yield
            v_stt("dve", TB[:], Xf1[:], r2_ap, lnb_bc[:], ALU.mult, ALU.add, [k1, r2_key, "lnb_bc"], ["TB"])
            yield
            for g in range(4):
                pe_mm(W2[:, g * 256:(g + 1) * 256], W_sT[:, g, :], TB[:, g * 256:(g + 1) * 256], True, True,
                      ["W_sT", "TB"], ["W2" + "ab"[g // 2]])
            for g in range(4):
                v_stt("dve", Xf0[:, g * 256:(g + 1) * 256], W2[:, g * 256:(g + 1) * 256], bsc[:, g:g + 1],
                      Xf0[:, g * 256:(g + 1) * 256], ALU.add, ALU.mult,
                      ["W2" + "ab"[g // 2], "bsc", k0], [k0])
                yield
            for nb in range(2):
                hT_mm(W2[:, nb * 512:(nb + 1) * 512], ["W2" + "ab"[nb]], lambda k: mT[:, k, :], mk, W_inc,
                      "W_inc", 2048 + nb * 512, 2048 + (nb + 1) * 512)
            act(Xf1[:], W2[:], AF.Tanh, WK, [k1], scale=0.5)
            yield
            v_stt("dve", Xf1[:], Xf1[:], 1.0, W2[:], ALU.add, ALU.mult, [k1] + WK, [k1])
            yield
            v_stt("dve", TB[:], Xf0[:], 0.25, Xf1[:], ALU.mult, ALU.mult, [k0, k1], ["TB"])
            yield
            for k in range(8):
                pe_tr(Tp[:, k * 128:(k + 1) * 128], TB[:, k * 128:(k + 1) * 128], ["TB", "ident"], ["Tp"])
            evac_mT()
            yield
            for nb in range(2):
                for k in range(8):
                    pe_mm(W2[:, nb * 512:(nb + 1) * 512], mT[:, k, :], W_outc[:, k, nb * 512:(nb + 1) * 512],
                          k == 0, k == 7, [("mT", k), "W_outc"], ["W2" + "ab"[nb]])
            ms_ap, ms_key = statcol()
            act(TB[:], W2[:], AF.Square, WK, ["TB", ms_key], scale=1.0 / 32.0, accum=ms_ap)
            yield
            r_ap, r_key = rstd_from(ms_ap, ms_key, EPS)
            yield
            v_stt("dve", Xf1[:], W2[:], r_ap, gateP[1][:], ALU.mult, ALU.mult, WK + [r_key, "gateP1"], [k1])
            yield
            v_tt("dve", Xa[:], Xf1[:], Xa[:], ALU.add, [k1, ka], [ka])
            yield
            dma("stX%d" % xa, "pool", q.y[q.row0 + i * 128: q.row0 + (i + 1) * 128, :], Xa[:], [ka], [])
            yield

        def gen_finish(q, b, embed=None):
            yield from gen_finish_tile(q, b, 0, embed)
            yield from gen_finish_tile(q, b, 1, embed)

        def filler_mm(n):
            for _ in range(n):
                pe_mm(W0[:, 0:512], ident[:], W_in[:, 0, 0:512], True, True, ["ident", "W_in"], ["W0a"])

        def interleave(gens, weights=None, fill=0):
            gens = list(gens)
            weights = list(weights) if weights else [1] * len(gens)
            alive = [True] * len(gens)
            while any(alive):
                for n_, g in enumerate(gens):
                    if not alive[n_]:
                        continue
                    for _ in range(weights[n_]):
                        try:
                            next(g)
                        except StopIteration:
                            alive[n_] = False
                            break
                if fill and not alive[0]:
                    filler_mm(fill)
                yield

        def run(g):
            for _ in g:
                pass

        def windowed(gen_list, width):
            pending = list(gen_list)
            active = []
            while pending or active:
                while pending and len(active) < width:
                    active.append(pending.pop(0))
                nxt = []
                for g in active:
                    try:
                        next(g)
                        nxt.append(g)
                    except StopIteration:
                        pass
                active = nxt
                yield

        def gen_front(q):
            yield from windowed([gen_phase1_tile(q, i) for i in range(q.ntiles)], 2)
            yield from gen_prep(q, 0)
            yield from gen_attn(q, 0)

        qs = make_seq(xs, ys, 0, S_T // 128, 0, True)
        qak = [("qabsT", h, tb) for h in range(4) for tb in range(2)]
        qpk = [("qpeT2", h, tb) for h in range(4) for tb in range(2)]
        qs.p1_opts = lambda i: dict(tbsel=[(TB, "TB"), (qpeT2, qpk)][i % 2], junk=(qabsT, qak))
        olk = lambda p: [("olT", p, h, tb) for h in range(4) for tb in range(2)]
        ada1 = gen_ada(1, [(olT[p][:].bitcast(F32), 512, "ldO%d" % p, olk(p)) for p in range(2)])
        rstd_mode[0] = "act"
        run(interleave([windowed([gen_phase1_tile(qs, i) for i in range(qs.ntiles)], 3), ada1], [1, 1]))
        rstd_mode[0] = "pool"
        build_gateP(0, 1)
        run(gen_cache_tiles(qs))
        nblk = qs.ntiles // 2

        def chain(*gs):
            for g in gs:
                yield from g

        run(gen_prep(qs, 0))
        run(gen_attn(qs, 0))
        if nblk > 1:
            run(gen_prep(qs, 1))
        for b in range(1, nblk):
            if b + 1 < nblk:
                emb = lambda tb, xb, b=b: prepA_tile(qs, b + 1, tb, xb=xb, u_bank=(Mb[:, 0:512], "Mb"))
                front = chain(gen_attn(qs, b), gen_prepQ(qs, b + 1))
            else:
                emb = None
                front = gen_attn(qs, b)
            run(interleave([front, gen_finish(qs, b - 1, emb)], [1, 1], fill=FILL))
        run(interleave([iter(()), gen_finish(qs, nblk - 1)], [1, 1], fill=FILL))
        build_gateP(1, 0)
        build_gateP(1, 1)
        nkt = qs.nk
        if nkt >= 34:
            XP = ckvT[:, 2304:4352].bitcast(F32)
            X.append(XP)
            xp_alias = [("ckvT", jk) for jk in range(18, 34)]
            TBP = ckvT[:, 256:1280]
            tbp_keys = [("ckvT", jk) for jk in range(2, 10)] + ["TBP"]
            tk.op("dve", lambda: nc.vector.memset(XP[:, 0:8], 0.0), [], xp_alias + ["X3"])
            p_opts = dict(xb=3, tbsel=(TBP, tbp_keys))
            overlap = True
        else:
            p_opts = {}
            overlap = False
        seqs = []
        for sq in range(P_SEQ):
            qp = make_seq(xp, yp, sq * P_T, P_T // 128, 1, False, p_row0=sq * P_T, gb0=nblk + sq)
            qp.u_off = 2 * (sq % 2)
            qp.mk_opts = dict(p_opts)
            qp.p1_opts = lambda i, p_opts=p_opts: dict(p_opts)
            seqs.append(qp)
        if overlap:
            run(gen_front(seqs[0]))
            for sq in range(P_SEQ):
                nxt = gen_front(seqs[sq + 1]) if sq + 1 < P_SEQ else iter(())
                run(interleave([nxt, gen_finish(seqs[sq], 0)], [1, 1], fill=FILL))
        else:
            for qp in seqs:
                run(gen_front(qp))
                run(interleave([iter(()), gen_finish(qp, 0)], [1, 1], fill=FILL))

        tk.wait_all("sp")
    return nc


def _host_constants():
    ident = np.eye(128, dtype=np.float32)
    T = 512
    bands = np.zeros((128, 20, 128), np.float32)
    t = np.arange(T)
    for g, w in enumerate(POOL_W):
        lo = np.clip(t - w // 2, 0, T)
        hi = np.clip(t + w - w // 2, 0, T)
        A = np.zeros((T, T), np.float64)
        for tt in range(T):
            A[tt, lo[tt]:hi[tt]] = 1.0 / (hi[tt] - lo[tt])
        I = np.eye(128)
        bands[:, g * 5 + 0, :] = (A[0:128, 0:128] - I).T
        bands[:, g * 5 + 1, :] = (A[128:256, 128:256] - I).T
        bands[:, g * 5 + 2, :] = (A[384:512, 384:512] - I).T
        bands[:, g * 5 + 3, :] = A[128:256, 0:128].T
        bands[:, g * 5 + 4, :] = A[128:256, 256:384].T
    half = 16
    inv = 10000.0 ** (-np.arange(half, dtype=np.float64) / half)
    tt = np.arange(S_T)
    row = (tt // 64).astype(np.float64)
    col = (tt % 64).astype(np.float64)
    ar = row[:, None] * inv[None, :]
    ac = col[:, None] * inv[None, :]
    C = np.concatenate([np.cos(ar), np.cos(ar), np.cos(ac), np.cos(ac)], axis=1)
    S = np.concatenate([-np.sin(ar), np.sin(ar), -np.sin(ac), np.sin(ac)], axis=1)
    rope = np.concatenate([C, S], axis=1).astype(np.float32)
    return ident, bands, rope


_NC_CACHE = {}


def make_in_maps(x_prompt, x_sample, cache_ckv, cache_kpe, c, c_ctx, w_ada, b_ada, norm_pre, norm_post,
                 w_in_ap, q_norm, w_uq, kv_norm, w_ukv, w_pool, pool_scale, w_out_ap,
                 w_in_c, sgu_ln_g, sgu_ln_b, w_s, b_s, w_out_c, ncores=NCORES, p_seq=P_SEQ):
    f = lambda a: np.ascontiguousarray(np.asarray(a, dtype=np.float32))
    x_prompt, x_sample, cache_ckv, cache_kpe, c, c_ctx = map(f, (x_prompt, x_sample, cache_ckv, cache_kpe, c, c_ctx))
    ident, bands, rope = _host_constants()
    s_t = x_sample.shape[1]
    w_ukv4 = f(w_ukv)[0].reshape(128, 4, 256)
    shared = dict(
        w_ada=f(w_ada),
        b_adaT=f(f(b_ada).reshape(2, 3, 8, 128).transpose(3, 0, 1, 2)),
        npre_c=f(f(norm_pre).reshape(2, 8, 128).transpose(2, 0, 1)),
        npost_c=f(f(norm_post).reshape(2, 8, 128).transpose(2, 0, 1)),
        w_in_ap=f(w_in_ap)[0],
        qnorm_c=f(f(q_norm)[0].reshape(2, 128).T),
        w_uq=f(w_uq)[0],
        kvn=f(kv_norm)[0].reshape(1, 128),
        w_ukT=f(w_ukv4[:, :, 0:128].transpose(2, 1, 0)),
        w_uv=f(w_ukv4[:, :, 128:256]),
        w_pool=f(f(w_pool)[0].transpose(1, 0, 2)),
        pscale=f(pool_scale)[0].reshape(1, 512),
        w_out_ap=f(w_out_ap)[0],
        w_in_c=f(w_in_c)[0],
        ln_g=f(sgu_ln_g)[0].reshape(1, D),
        ln_b=f(sgu_ln_b)[0].reshape(1, D),
        w_sT=f(f(w_s)[0].transpose(2, 0, 1)),
        b_s_c=f(f(b_s)[0].T),
        w_out_c=f(w_out_c)[0],
        ident=ident, bands=bands, rope_tab=f(rope[:s_t]),
    )
    in_maps = []
    for core in range(ncores):
        cond = np.stack([c[core], c_ctx], axis=0)
        m = dict(shared)
        m["xs"] = x_sample[core]
        m["xp"] = f(x_prompt[core * p_seq:(core + 1) * p_seq].reshape(p_seq * P_T, D))
        m["cckv"] = cache_ckv[core, 0]
        m["ckpe"] = cache_kpe[core, 0]
        m["condT"] = f(cond.reshape(2, 8, 128).transpose(2, 1, 0))
        in_maps.append(m)
    return in_maps


def assemble(outs, ncores=NCORES, p_seq=P_SEQ):
    y_sample = np.stack([outs[i]["ys"] for i in range(ncores)], axis=0).astype(np.float32)
    y_prompt = np.concatenate([outs[i]["yp"].reshape(p_seq, P_T, D) for i in range(ncores)], axis=0).astype(np.float32)
    new_ckv = np.concatenate([outs[i]["nckv"].reshape(p_seq, 1, P_T, 128) for i in range(ncores)], axis=0).astype(np.float32)
    new_kpe = np.concatenate([outs[i]["nkpe"].reshape(p_seq, 1, P_T, 64) for i in range(ncores)], axis=0).astype(np.float32)
    return (y_prompt, y_sample, new_ckv, new_kpe)


def kernel(**inputs):
    in_maps = make_in_maps(**inputs)
    if "nc" not in _NC_CACHE:
        _NC_CACHE["nc"] = build_program()
    res = run_bass_kernel_spmd(_NC_CACHE["nc"], in_maps, core_ids=list(range(NCORES)))
    return assemble(res.results)
```

```python
import math
from contextlib import ExitStack

import numpy as np
import concourse.bass as bass
import concourse.mybir as mybir
from concourse.bass_utils import run_bass_kernel_spmd

F32 = mybir.dt.float32
BF16 = mybir.dt.bfloat16
AF = mybir.ActivationFunctionType
ALU = mybir.AluOpType

D = 1024
NCORES = 8
S_T = 4096
P_SEQ = 4
P_T = 256
PAST = 256
EPS = 1e-6
SM_SCALE = 192.0 ** -0.5
POOL_W = (2, 4, 8, 16)
C1 = math.sqrt(2.0 / math.pi)
C2 = 0.044715
import os as _os
FILL = int(_os.environ.get("KFILL", "3"))
QL0, KV0, KPE0, GA0, PI0, GB0 = 0, 256, 384, 448, 960, 1472


class Tracker:
    def __init__(self, nc, es):
        self.nc = nc
        self.es = es
        self.engs = {}
        self.last_write = {}
        self.readers = {}
        self.excl = set(["W0a", "W0b", "W1a", "W1b", "W2a", "W2b", "Tp", "Mb"])

    def add(self, name, eng, unit):
        sem = self.es.enter_context(self.nc.semaphore("s_" + name))
        self.engs[name] = dict(eng=eng, sem=sem, count=0, unit=unit, seen={})

    def op(self, ename, fn, reads=(), writes=(), issuer=None):
        E = self.engs[ename]
        iname = issuer or ename
        I = self.engs[iname]
        is_dma = issuer is not None
        deps = {}

        def add(d, raw):
            if d is None:
                return
            dn, idx = d
            if dn == ename and not is_dma:
                if ename == "pe":
                    return
            if deps.get(dn, 0) < idx:
                deps[dn] = idx

        for r in reads:
            add(self.last_write.get(r), True)
            if r in self.excl:
                for dn, idx in self.readers.get(r, {}).items():
                    if dn != ename:
                        add((dn, idx), False)
        for w in writes:
            add(self.last_write.get(w), False)
            for dn, idx in self.readers.get(w, {}).items():
                add((dn, idx), False)
        for dn, idx in deps.items():
            De = self.engs[dn]
            if De["unit"] == 16:
                idx = De["count"]
            if De["unit"] == 16:
                De["cwait"] = True
            if I["seen"].get(dn, 0) < idx:
                I["eng"].wait_ge(De["sem"], idx * De["unit"])
                I["seen"][dn] = idx
        if is_dma and E.get("cwait"):
            if I["seen"].get(ename, 0) < E["count"]:
                I["eng"].wait_ge(E["sem"], E["count"] * 16)
                I["seen"][ename] = E["count"]
            E["cwait"] = False
        ins = fn()
        E["count"] += 1
        ins.then_inc(E["sem"], E["unit"])
        me = (ename, E["count"])
        for r in reads:
            rd = self.readers.setdefault(r, {})
            rd[ename] = E["count"]
        for w in writes:
            self.last_write[w] = me
            self.readers[w] = {}
        return me

    def wait_all(self, iname):
        I = self.engs[iname]
        for dn, De in self.engs.items():
            if De["count"] > 0 and dn != iname:
                I["eng"].wait_ge(De["sem"], De["count"] * De["unit"])
                I["seen"][dn] = De["count"]


class _Stop(Exception):
    pass


def build_program(S_T=S_T, P_SEQ=P_SEQ, stop=0):
    nc = bass.Bass("TRN2", target_bir_lowering=False)

    def din(name, shape):
        return nc.dram_tensor(name, list(shape), F32, kind="ExternalInput").ap()

    def dout(name, shape):
        return nc.dram_tensor(name, list(shape), F32, kind="ExternalOutput").ap()

    xs = din("xs", [S_T, D])
    xp = din("xp", [P_SEQ * P_T, D])
    cckv = din("cckv", [PAST, 128])
    ckpe = din("ckpe", [PAST, 64])
    condT = din("condT", [128, 8, 2])
    w_ada = din("w_ada", [2, D, 3 * D])
    b_adaT = din("b_adaT", [128, 2, 3, 8])
    npre_c = din("npre_c", [128, 2, 8])
    npost_c = din("npost_c", [128, 2, 8])
    w_in_ap = din("w_in_ap", [D, 1984])
    qnorm_c = din("qnorm_c", [128, 2])
    w_uq = din("w_uq", [256, 768])
    kvn = din("kvn", [1, 128])
    w_ukT = din("w_ukT", [128, 4, 128])
    w_uv = din("w_uv", [128, 4, 128])
    w_pool = din("w_pool", [128, 4, 128])
    pscale = din("pscale", [1, 512])
    w_out_ap = din("w_out_ap", [D, D])
    w_in_c = din("w_in_c", [D, 3 * D])
    ln_g = din("ln_g", [1, D])
    ln_b = din("ln_b", [1, D])
    w_sT = din("w_sT", [128, 4, 128])
    b_s_c = din("b_s_c", [128, 4])
    w_out_c = din("w_out_c", [D, D])
    ident_d = din("ident", [128, 128])
    bands_d = din("bands", [128, 20, 128])
    rope_d = din("rope_tab", [S_T, 128])

    ys = dout("ys", [S_T, D])
    yp = dout("yp", [P_SEQ * P_T, D])
    nckv = dout("nckv", [P_SEQ * P_T, 128])
    nkpe = dout("nkpe", [P_SEQ * P_T, 64])

    with ExitStack() as es:
        def sb(name, shape, dt):
            return es.enter_context(nc.sbuf_tensor("sb_" + name, list(shape), dt))

        def ps(name, shape, dt):
            return es.enter_context(nc.psum_tensor("ps_" + name, list(shape), dt))

        tk = Tracker(nc, es)
        tk.add("pe", nc.tensor, 1)
        tk.add("act", nc.scalar, 1)
        tk.add("dve", nc.vector, 1)
        tk.add("pool", nc.gpsimd, 1)
        _streams = set()

        def stream(name):
            if name not in _streams:
                tk.add(name, None, 16)
                _streams.add(name)
            return name

        W_in = sb("W_in", [128, 8, 1984], BF16)
        W_uq = sb("W_uq", [128, 2, 768], BF16)
        W_ukT = sb("W_ukT", [128, 4, 128], BF16)
        W_uv = sb("W_uv", [128, 4, 128], BF16)
        W_pool = sb("W_pool", [128, 4, 128], BF16)
        W_out = sb("W_out", [128, 8, 1024], BF16)
        W_inc = sb("W_inc", [128, 8, 3072], BF16)
        W_sT = sb("W_sT", [128, 4, 128], BF16)
        W_outc = sb("W_outc", [128, 8, 1024], BF16)
        ckvT = sb("ckvT", [128, 34 * 128], BF16)
        ckv_aug = sb("ckv_aug", [128, 34, 130], BF16)
        kpeT2 = sb("kpeT2", [128, 17 * 128], BF16)
        gateP = [sb("gateP0", [128, D], F32), sb("gateP1", [128, D], F32)]
        lng_bc = sb("lng_bc", [128, D], F32)
        lnb_bc = sb("lnb_bc", [128, D], F32)
        kvn_bc = sb("kvn_bc", [128, 128], F32)
        bands = sb("bands", [128, 20, 128], BF16)
        ident = sb("ident", [128, 128], BF16)
        modc = sb("modc", [128, 2, 3, 8, 2], F32)
        Gc = sb("Gc", [128, 2, 8, 2], F32)
        gPc = sb("gPc", [128, 2, 8, 2], F32)
        badac = sb("badac", [128, 2, 3, 8], F32)
        nprec = sb("nprec", [128, 2, 8], F32)
        npostc = sb("npostc", [128, 2, 8], F32)
        qnc = sb("qnc", [128, 2], F32)
        bsc = sb("bsc", [128, 4], F32)
        scT = sb("scT", [128, 8, 2], F32)
        mhalf = sb("mhalf", [128, 1], F32)
        epsc = sb("epsc", [128, 1], F32)
        kpe_c = sb("kpe_c", [128, 2, 64], BF16)
        stat = sb("stat", [128, 64], F32)
        X = [sb("X%d" % i, [128, D], F32) for i in range(3)]
        TB = sb("TB", [128, D], BF16)
        hT = [sb("hT%d" % i, [128, 8, 256], BF16) for i in range(2)]
        mT = sb("mT", [128, 8, 128], BF16)
        q_all = sb("q_all", [128, 4, 256], BF16)
        qn = q_all[:, 2, :]
        qnT = q_all[:, 3, :].rearrange("p (c k) -> p c k", c=2)
        qnopeT = [sb("qnopeT0", [128, 128], BF16)]
        qabsT = sb("qabsT", [128, 1024], BF16)
        qpeT2 = sb("qpeT2", [128, 1024], BF16)
        u_ring = [sb("u%d" % i, [128, 512], BF16) for i in range(5)]
        PT = [sb("PT%d" % i, [128, 512], BF16) for i in range(2)]
        olT = [sb("olT%d" % i, [128, 1024], BF16) for i in range(2)]
        oln = [sb("oln0", [128, 128], BF16)]
        dT = sb("dT", [128, 512], BF16)
        ropeT = [sb("ropeT0", [128, 128], F32)]
        rtmp = sb("rtmp", [128, 64], F32)
        rtmp2 = sb("rtmp2", [128, 64], F32)
        ckv_f = [sb("ckv_f0", [128, 128], F32)]
        kpe_f = [sb("kpe_f0", [128, 64], F32)]
        kpe_bfs = [sb("kpe_bf0", [128, 128], BF16)] * 2
        bnst = sb("bnst", [128, 2, 6], F32)
        bnag = sb("bnag", [128, 2], F32)
        W0 = ps("W0", [128, D], F32)
        W1 = ps("W1", [128, D], F32)
        W2 = ps("W2", [128, D], F32)
        Tp = ps("Tp", [128, D], BF16)
        Mb = ps("Mb", [128, 512], F32)

        def pe_mm(out, lhsT, rhs, start, stop, reads, writes, skip=False):
            if skip:
                return tk.op("pe", lambda: nc.tensor.matmul(out, lhsT, rhs, start=start, stop=stop,
                                                            skip_group_check=True), reads, writes)
            return tk.op("pe", lambda: nc.tensor.matmul(out, lhsT, rhs, start=start, stop=stop),
                         reads, writes)

        def pe_tr(out, in_, reads, writes, idn=None):
            idn = ident if idn is None else idn
            return tk.op("pe", lambda: nc.tensor.transpose(out, in_, idn[:]), reads, writes)

        def act(out, in_, func, reads, writes, scale=1.0, bias=None, accum=None):
            kw = {}
            if bias is not None:
                kw["bias"] = bias
            if accum is not None:
                kw["accum_out"] = accum
            return tk.op("act", lambda: nc.scalar.activation(out=out, in_=in_, func=func, scale=scale, **kw),
                         reads, writes)

        def veng(e):
            return nc.vector if e == "dve" else nc.gpsimd

        def v_copy(e, out, in_, reads, writes):
            return tk.op(e, lambda: veng(e).tensor_copy(out=out, in_=in_), reads, writes)

        def v_ts(e, out, in0, s1, s2, op0, op1, reads, writes):
            if s2 is None:
                return tk.op(e, lambda: veng(e).tensor_scalar(out=out, in0=in0, scalar1=s1, scalar2=None, op0=op0),
                             reads, writes)
            return tk.op(e, lambda: veng(e).tensor_scalar(out=out, in0=in0, scalar1=s1, scalar2=s2, op0=op0, op1=op1),
                         reads, writes)

        def v_tt(e, out, in0, in1, op, reads, writes):
            return tk.op(e, lambda: veng(e).tensor_tensor(out=out, in0=in0, in1=in1, op=op), reads, writes)

        def v_stt(e, out, in0, scalar, in1, op0, op1, reads, writes):
            return tk.op(e, lambda: veng(e).scalar_tensor_tensor(out=out, in0=in0, scalar=scalar, in1=in1,
                                                                  op0=op0, op1=op1), reads, writes)

        def ckpt(n):
            if stop == n:
                raise _Stop()

        def dma(strm, issuer, out, in_, reads, writes):
            q = nc.sync if issuer == "sp" else nc.gpsimd
            return tk.op(stream(strm), lambda: q.dma_start(out=out, in_=in_), reads, writes,
                         issuer=issuer)

        tk.add("sp", nc.sync, 1)

        _stat_i = [0]

        def statcol():
            i = _stat_i[0] % 60
            _stat_i[0] += 1
            return stat[:, i:i + 1], ("stat", i)

        rstd_mode = ["pool"]

        def rstd_from(ms_ap, ms_key, eps):
            if rstd_mode[0] == "act":
                t_ap, t_key = statcol()
                act(t_ap, ms_ap, AF.Ln, [ms_key, "epsc"], [t_key], bias=epsc[:])
                r_ap, r_key = statcol()
                act(r_ap, t_ap, AF.Exp, [t_key], [r_key], scale=-0.5)
                return r_ap, r_key
            t_ap, t_key = statcol()
            v_ts("pool", t_ap, ms_ap, eps, None, ALU.add, None, [ms_key], [t_key])
            r_ap, r_key = statcol()
            v_tt("pool", r_ap, t_ap, mhalf[:], ALU.pow, [t_key, "mhalf"], [r_key])
            return r_ap, r_key

        tk.op("pool", lambda: nc.gpsimd.memset(mhalf[:], -0.5), [], ["mhalf"])
        tk.op("pool", lambda: nc.gpsimd.memset(epsc[:], EPS), [], ["epsc"])
        tk.op("pool", lambda: nc.gpsimd.memset(kpe_bfs[0][:], 0.0), [], ["kpe_bf"])
        tk.op("dve", lambda: nc.vector.memset(ckv_aug[:, :, 128:130], 1.0), [], ["ckv_aug_ones"])

        dma("c_small", "sp", scT[:], condT[:], [], ["scT"])
        dma("c_small", "sp", badac[:], b_adaT[:], [], ["badac"])
        dma("c_small", "sp", nprec[:], npre_c[:], [], ["nprec"])
        dma("c_small", "sp", npostc[:], npost_c[:], [], ["npostc"])
        dma("c_small", "sp", qnc[:], qnorm_c[:], [], ["qnc"])
        dma("c_small", "sp", bsc[:], b_s_c[:], [], ["bsc"])
        dma("c_small", "sp", kvn_bc[:], kvn.partition_broadcast(128), [], ["kvn_bc"])
        dma("c_small", "sp", lng_bc[:], ln_g.partition_broadcast(128), [], ["lng_bc"])
        dma("c_small", "sp", lnb_bc[:], ln_b.partition_broadcast(128), [], ["lnb_bc"])
        dma("c_cast", "pool", ident[:], ident_d[:], [], ["ident"])
        dma("c_cast", "pool", bands[:], bands_d[:], [], ["bands"])

        act(stat[:, 48:64], scT[:].rearrange("p k j -> p (k j)"), AF.Tanh, ["scT"], ["sc_th"], scale=0.5)
        v_stt("dve", scT[:].rearrange("p k j -> p (k j)"), stat[:, 48:64], 1.0,
              scT[:].rearrange("p k j -> p (k j)"), ALU.add, ALU.mult, ["sc_th", "scT"], ["scT"])
        v_ts("dve", scT[:].rearrange("p k j -> p (k j)"), scT[:].rearrange("p k j -> p (k j)"), 0.5, None,
             ALU.mult, None, ["scT"], ["scT"])
        _stat_i[0] = 0

        def gen_ada(l, stages):
            ns = len(stages)
            cnt = 0
            for k in range(8):
                for part in range(3):
                    W_ = stages[0][1]
                    for sub in range(D // W_):
                        st_ap, _, strm, keys = stages[cnt % ns]
                        cnt += 1
                        c0 = part * D + sub * W_
                        dma(strm, "sp", st_ap, w_ada[l, k * 128:(k + 1) * 128, c0:c0 + W_], [], keys)
                        for c_ in range(W_ // 128):
                            cc = sub * (W_ // 128) + c_
                            col = ((l * 3 + part) * 8 + cc) * 2
                            pe_mm(Mb[:, col:col + 2], st_ap[:, c_ * 128:(c_ + 1) * 128], scT[:, k, :],
                                  (k == 0 and part == 0 and cc == 0), k == 7,
                                  keys + ["scT"], ["Mb"], skip=True)
                        yield
            lo = l * 48
            v_copy("dve", modc[:, l].rearrange("p a c j -> p (a c j)"), Mb[:, lo:lo + 48], ["Mb"], [("modc", l)])
            for j in range(2):
                v_tt("dve", modc[:, l, :, :, j], modc[:, l, :, :, j], badac[:, l], ALU.add,
                     [("modc", l), "badac"], [("modc", l)])
            for j in range(2):
                v_stt("dve", Gc[:, l, :, j], modc[:, l, 1, :, j], 1.0, nprec[:, l, :], ALU.add, ALU.mult,
                      [("modc", l), "nprec"], [("Gc", l)])
                v_tt("dve", gPc[:, l, :, j], modc[:, l, 2, :, j], npostc[:, l, :], ALU.mult,
                     [("modc", l), "npostc"], [("gPc", l)])
            yield

        def build_gateP(j, l):
            tk.op("dve", lambda: nc.vector.memset(X[2][:, 0:128], 1.0), [], ["X2"])
            for cc in range(8):
                xb = cc % 2
                v_ts("dve", X[xb][:, 0:128], ident[:], gPc[:, l, cc, j:j + 1], None, ALU.mult, None,
                     ["ident", ("gPc", l)], ["X%d" % xb])
                pe_mm(W2[:, cc * 128:(cc + 1) * 128], X[2][:, 0:128], X[xb][:, 0:128], True, True,
                      ["X2", "X%d" % xb], ["W2" + "ab"[cc // 4]])
            act(gateP[l][:], W2[:], AF.Identity, ["W2a", "W2b"], ["gateP%d" % l])

        for _ in gen_ada(0, [(X[i][:], D, "ldX%d" % i, ["X%d" % i]) for i in range(3)]):
            pass

        w_in_v = w_in_ap.rearrange("(k p) n -> p k n", p=128)
        dma("w_inkv", "pool", W_in[:, :, KV0:KV0 + 192], w_in_v[:, :, KV0:KV0 + 192], [], ["W_inkv"])
        if S_T // 128 + 2 <= 34:
            for c in range(2):
                jk = S_T // 128 + c
                dma("c_cache", "pool", ckv_aug[:, jk, 0:128], cckv[c * 128:(c + 1) * 128, :], [], [("ckv_aug", jk)])
                dma("c_cache", "pool", kpe_c[:, c, :], ckpe[c * 128:(c + 1) * 128, :], [], [("kpe_c", c)])
        for k in range(8):
            dma("w_in", "pool", W_in[:, k, 0:KV0], w_in_v[:, k, 0:KV0], [], ["W_in"])
            dma("w_in", "pool", W_in[:, k, KV0 + 192:1984], w_in_v[:, k, KV0 + 192:1984], [], ["W_in"])
        dma("w_small", "pool", W_uq[:], w_uq.rearrange("(k p) n -> p k n", p=128), [], ["W_uq"])
        dma("w_small", "pool", W_ukT[:], w_ukT[:], [], ["W_ukT"])
        dma("w_small", "pool", W_uv[:], w_uv[:], [], ["W_uv"])
        dma("w_small", "pool", W_sT[:], w_sT[:], [], ["W_sT"])
        dma("ldX0", "sp", X[0][:, 0:512], w_pool.rearrange("p g c -> p (g c)"), [], ["X0"])
        dma("ldX1", "sp", X[1][:, 0:512], pscale.partition_broadcast(128), [], ["X1"])
        v_tt("dve", W_pool[:].rearrange("p g c -> p (g c)"), X[0][:, 0:512], X[1][:, 0:512], ALU.mult,
             ["X0", "X1"], ["W_pool"])
        w_out_v = w_out_ap.rearrange("(k p) n -> p k n", p=128)
        for k in range(8):
            dma("w_out", "pool", W_out[:, k, :], w_out_v[:, k, :], [], ["W_out"])
        w_inc_v = w_in_c.rearrange("(k p) n -> p k n", p=128)
        for k in range(8):
            dma("w_inc", "pool", W_inc[:, k, :], w_inc_v[:, k, :], [], ["W_inc"])
        w_outc_v = w_out_c.rearrange("(k p) n -> p k n", p=128)
        for k in range(8):
            dma("w_outc", "pool", W_outc[:, k, :], w_outc_v[:, k, :], [], ["W_outc"])

        build_gateP(0, 0)
        xrot = [0]
        tslot = [0]
        evac_flip = [0]
        tbrot = [0]
        kvrot = [0]
        TBs = [(TB, "TB"), (TB, "TB")]

        def tslot_next():
            s = tslot[0] % 8
            tslot[0] += 1
            return s

        def klist(k):
            return list(k) if isinstance(k, list) else [k]

        def norm_to_T(src_ap, src_keys, width_scale, tb_ap, tb_key, junk=None):
            ms_ap, ms_key = statcol()
            junk_ap, junk_keys = (tb_ap, klist(tb_key)) if junk is None else junk
            tk.op("dve", lambda: nc.vector.scalar_tensor_tensor(
                out=junk_ap[:], in0=src_ap, scalar=width_scale * width_scale, in1=src_ap,
                op0=ALU.mult, op1=ALU.mult, accum_out=ms_ap), src_keys, junk_keys + [ms_key])
            r_ap, r_key = rstd_from(ms_ap, ms_key, EPS)
            v_ts("dve", tb_ap[:], src_ap, r_ap, None, ALU.mult, None, src_keys + [r_key], klist(tb_key))

        def T_to_hT(dest, dest_key, l, j, tb_ap, tb_key):
            for k in range(8):
                pe_tr(Tp[:, k * 128:(k + 1) * 128], tb_ap[:, k * 128:(k + 1) * 128], klist(tb_key) + ["ident"], ["Tp"])
            evac_flip[0] += 1
            for k in range(8):
                g_ap = Gc[:, l, k, j:j + 1]
                s_ap = modc[:, l, 0, k, j:j + 1]
                if True:
                    v_ts("dve", dest(k), Tp[:, k * 128:(k + 1) * 128], g_ap, s_ap, ALU.mult, ALU.add,
                         ["Tp", ("Gc", l), ("modc", l)], [(dest_key, k)])
                else:
                    act(dest(k), Tp[:, k * 128:(k + 1) * 128], AF.Identity, ["Tp", ("Gc", l), ("modc", l)],
                        [(dest_key, k)], scale=g_ap, bias=s_ap)

        def evac_mT():
            evac_flip[0] += 1
            src = Tp[:].rearrange("p (k c) -> p k c", k=8)
            if True:
                v_copy("dve", mT[:], src, ["Tp"], [("mT", k) for k in range(8)])
            else:
                act(mT[:], src, AF.Identity, ["Tp"], [("mT", k) for k in range(8)])

        def gen_make_hT(x_rows, dest, dest_key, j, xb=None, tbsel=None, junk=None):
            if xb is None:
                xb = xrot[0] % 3
                xrot[0] += 1
            if tbsel is None:
                tb_ap, tb_key = TBs[tbrot[0] % 2]
                tbrot[0] += 1
            else:
                tb_ap, tb_key = tbsel
            dma("ldX%d" % xb, "sp", X[xb][:], x_rows, [], ["X%d" % xb])
            norm_to_T(X[xb][:], ["X%d" % xb], 1.0 / 32.0, tb_ap, tb_key, junk=junk)
            T_to_hT(dest, dest_key, 0, j, tb_ap, tb_key)
            yield

        def rope_apply(src, src_keys, dst, dst_keys, tab, tab_key):
            s3 = src.rearrange("p (a f) -> p a f", a=2)
            t3 = rtmp[:].rearrange("p (a f) -> p a f", a=2)
            S3 = tab[:, 64:128].rearrange("p (a f) -> p a f", a=2)
            v_tt("dve", t3[:, :, 0:16], s3[:, :, 16:32], S3[:, :, 0:16], ALU.mult, src_keys + [tab_key], ["rtmp_a"])
            v_tt("dve", t3[:, :, 16:32], s3[:, :, 0:16], S3[:, :, 16:32], ALU.mult, src_keys + [tab_key], ["rtmp_b"])
            v_tt("dve", rtmp2[:], src, tab[:, 0:64], ALU.mult, src_keys + [tab_key], ["rtmp2"])
            v_tt("dve", dst, rtmp2[:], rtmp[:], ALU.add, ["rtmp2", "rtmp_a", "rtmp_b"], dst_keys)

        def hT_mm(out, out_keys, lhs_of_k, lhs_keys_of_k, Wt, wkey, c0, c1):
            for k in range(8):
                pe_mm(out, lhs_of_k(k), Wt[:, k, c0:c1], k == 0, k == 7, [lhs_keys_of_k(k), wkey], out_keys)

        ATT_BANKS = [(W0, 0, "W0a"), (W0, 512, "W0b"), (W1, 0, "W1a"), (W1, 512, "W1b")]

        class Seq:
            pass

        def make_seq(x_dram, y_dram, row0, ntiles, j, is_sample, p_row0=0, gb0=0):
            q = Seq()
            q.x, q.y, q.row0, q.ntiles, q.j, q.is_sample, q.p_row0 = x_dram, y_dram, row0, ntiles, j, is_sample, p_row0
            q.nk = ntiles + (2 if is_sample else 0)
            q.last = ntiles - 1
            q.u_done = set()
            q.prepA_done = set()
            q.u_off = 0
            q.mk_opts = {}
            q.p1_opts = lambda i: {}
            q.gb0 = gb0
            if is_sample:
                q.p1_dest = lambda i: ((i % 4) // 2, i % 2)
            else:
                q.p1_dest = lambda i: (gb0 % 2, i % 2)
            return q

        def xrows(q, i):
            return q.x[q.row0 + i * 128: q.row0 + (i + 1) * 128, :]

        def load_rope(i):
            rb = 0
            dma("ldR%d" % rb, "sp", ropeT[rb][:], rope_d[i * 128:(i + 1) * 128, :], [], ["ropeT%d" % rb])
            return ropeT[rb], "ropeT%d" % rb

        def put_key_tile(jk):
            s = tslot_next()
            pe_tr(Tp[:, s * 128:(s + 1) * 128], ckv_aug[:, jk, 0:128], [("ckv_aug", jk), "ident"], ["Tp"])
            v_copy("dve", ckvT[:, jk * 128:(jk + 1) * 128], Tp[:, s * 128:(s + 1) * 128], ["Tp"],
                   [("ckvT", jk)])
            s = tslot_next()
            hf = jk % 2
            pe_tr(Tp[:, s * 128:(s + 1) * 128], kpe_bfs[hf][:], ["kpe_bf", "ident"], ["Tp"])
            v_copy("dve", kpeT2[hf * 64:(hf + 1) * 64, (jk // 2) * 128:(jk // 2 + 1) * 128],
                   Tp[hf * 64:(hf + 1) * 64, s * 128:(s + 1) * 128], ["Tp"], [("kpeT2", jk)])

        def gen_phase1_tile(q, i):
            hbuf, hreg = q.p1_dest(i)
            dest = lambda k: hT[hbuf][:, k, hreg * 128:(hreg + 1) * 128]
            dkey = ("hT", hbuf, hreg)
            yield from gen_make_hT(xrows(q, i), dest, dkey, q.j, **q.p1_opts(i))
            KVt, kc, kkey = ATT_BANKS[kvrot[0] % 4]
            kvrot[0] += 1
            KV = KVt[:, kc:kc + 192]
            hT_mm(KV, [kkey], dest, lambda k: (dkey, k), W_in, "W_inkv", KV0, KV0 + 192)
            ms_ap, ms_key = statcol()
            act(qn[:, 0:128], KV[:, 0:128], AF.Square, [kkey], ["qn", ms_key],
                scale=128.0 ** -0.5, accum=ms_ap)
            r_ap, r_key = rstd_from(ms_ap, ms_key, EPS)
            yield
            cb = 0
            v_stt("dve", ckv_f[cb][:], KV[:, 0:128], r_ap, kvn_bc[:], ALU.mult, ALU.mult,
                  [kkey, r_key, "kvn_bc"], ["ckv_f%d" % cb])
            hf = i % 2
            if q.is_sample:
                tab, tabk = load_rope(i)
                rope_apply(KV[:, 128:192], [kkey], kpe_bfs[hf][:, hf * 64:(hf + 1) * 64], ["kpe_bf"], tab, tabk)
                act(ckv_aug[:, i, 0:128], ckv_f[cb][:], AF.Identity, ["ckv_f%d" % cb], [("ckv_aug", i)])
            else:
                v_copy("dve", kpe_f[cb][:], KV[:, 128:192], [kkey], ["kpe_f%d" % cb])
                act(ckv_aug[:, i, 0:128], ckv_f[cb][:], AF.Identity, ["ckv_f%d" % cb], [("ckv_aug", i)])
                act(kpe_bfs[hf][:, hf * 64:(hf + 1) * 64], kpe_f[cb][:], AF.Identity, ["kpe_f%d" % cb],
                    ["kpe_bf"])
                r0 = q.p_row0 + i * 128
                dma("st_ckv%d" % cb, "pool", nckv[r0:r0 + 128, :], ckv_f[cb][:], ["ckv_f%d" % cb], [])
                dma("st_kpe%d" % cb, "pool", nkpe[r0:r0 + 128, :], kpe_f[cb][:], ["kpe_f%d" % cb], [])
            put_key_tile(i)
            yield

        def gen_cache_tiles(q):
            for c in range(2):
                jk = q.ntiles + c
                hf = jk % 2
                v_copy("dve", kpe_bfs[hf][:, hf * 64:(hf + 1) * 64], kpe_c[:, c, :], [("kpe_c", c)], ["kpe_bf"])
                put_key_tile(jk)
                yield

        def prepA_tile(q, b, tb, xb=None, u_bank=None):
            i = 2 * b + tb
            hb = (q.gb0 + b) % 2
            lhs = lambda k: hT[hb][:, k, tb * 128:(tb + 1) * 128]
            dkey = ("hT", hb, tb)
            lkeys = lambda k: (dkey, k)
            opts = dict(q.mk_opts)
            if xb is not None:
                opts["xb"] = xb
            for _ in gen_make_hT(xrows(q, i), lhs, dkey, q.j, **opts):
                pass
            if i not in q.u_done:
                if u_bank is None:
                    U, ukey = W0[:, 0:512], "W0a"
                else:
                    U, ukey = u_bank
                hT_mm(U, [ukey], lhs, lkeys, W_in, "W_in", PI0, PI0 + 512)
                ui = (i + q.u_off) % 5
                act(u_ring[ui][:], U, AF.Identity, [ukey], [("u", ui)])
                q.u_done.add(i)
            q.prepA_done.add((b, tb))

        def gen_prepQ_tile(q, b, tb):
            i = 2 * b + tb
            hb = (q.gb0 + b) % 2
            lhs = lambda k: hT[hb][:, k, tb * 128:(tb + 1) * 128]
            dkey = ("hT", hb, tb)
            lkeys = lambda k: (dkey, k)
            while (b, tb) not in q.prepA_done:
                yield
            QL = W0[:, 512:768]
            hT_mm(QL, ["W0b"], lhs, lkeys, W_in, "W_in", QL0, QL0 + 256)
            ms_ap, ms_key = statcol()
            act(qn[:], QL, AF.Square, ["W0b"], ["qn", ms_key], scale=1.0 / 16.0, accum=ms_ap)
            r_ap, r_key = rstd_from(ms_ap, ms_key, EPS)
            act(qn[:], QL, AF.Identity, ["W0b", r_key], ["qn"], scale=r_ap)
            yield
            for c in range(2):
                pe_tr(Tp[:, c * 128:(c + 1) * 128], qn[:, c * 128:(c + 1) * 128], ["qn", "ident"], ["Tp"])
            v_tt("dve", qnT, Tp[:, 0:256].rearrange("p (c k) -> p c k", c=2),
                 qnc[:, 0:2].unsqueeze(2).to_broadcast([128, 2, 128]), ALU.mult,
                 ["Tp", "qnc"], [("qnT", 0), ("qnT", 1)])
            for c in range(2):
                pe_mm(W1[:, 0:512], qnT[:, c, :], W_uq[:, c, 0:512], c == 0, c == 1,
                      [("qnT", c), "W_uq"], ["W1a"])
            for c in range(2):
                pe_mm(W1[:, 512:768], qnT[:, c, :], W_uq[:, c, 512:768], c == 0, c == 1,
                      [("qnT", c), "W_uq"], ["W1b"])
            if q.is_sample:
                tab, tabk = load_rope(i)
            QK = ["W1a", "W1b"]
            qv = W1[:, 0:768].rearrange("p (h c) -> p h c", h=4)
            v_copy("dve", q_all[:, :, 0:128], qv[:, :, 0:128], QK, ["q_all", "qn", ("qnT", 0), ("qnT", 1)])
            src = qv[:, :, 128:192]
            if q.is_sample:
                tA = PT[0][:].bitcast(F32).rearrange("p (h c) -> p h c", h=4)
                tB = PT[1][:].bitcast(F32).rearrange("p (h c) -> p h c", h=4)
                s4 = src.rearrange("p h (a f) -> p h a f", a=2)
                t4 = tA.rearrange("p h (a f) -> p h a f", a=2)
                Sb = tab[:, 64:128].rearrange("p (a f) -> p a f", a=2).unsqueeze(1).to_broadcast([128, 4, 2, 32])
                Cb = tab[:, 0:64].unsqueeze(1).to_broadcast([128, 4, 64])
                v_tt("dve", t4[:, :, :, 0:16], s4[:, :, :, 16:32], Sb[:, :, :, 0:16], ALU.mult, QK + [tabk], ["PT0"])
                v_tt("dve", t4[:, :, :, 16:32], s4[:, :, :, 0:16], Sb[:, :, :, 16:32], ALU.mult, QK + [tabk], ["PT0"])
                v_tt("dve", tB, src, Cb, ALU.mult, QK + [tabk], ["PT1"])
                v_tt("dve", q_all[:, :, 128:192], tB, tA, ALU.add, ["PT0", "PT1"], ["q_all"])
            else:
                v_copy("dve", q_all[:, :, 128:192], src, QK, ["q_all"])
            v_copy("pool", q_all[:, :, 192:256], q_all[:, :, 128:192], ["q_all"], ["q_all"])
            yield
            for h in range(4):
                pe_tr(Tp[:, h * 128:(h + 1) * 128], q_all[:, h, 0:128], ["q_all", "ident"], ["Tp"])
            for h in range(4):
                pe_tr(Tp[:, (4 + h) * 128:(5 + h) * 128], q_all[:, h, 128:256], ["q_all", "ident"], ["Tp"])
            qnT_all = PT[0][:]
            v_copy("dve", qnT_all, Tp[:, 0:512], ["Tp"], ["PT0"])
            qpe_dst = qpeT2[:].rearrange("p (h t c) -> p h t c", h=4, t=2)[:, :, tb, :]
            v_copy("dve", qpe_dst, Tp[:, 512:1024].rearrange("p (h c) -> p h c", h=4), ["Tp"],
                   [("qpeT2", h, tb) for h in range(4)])
            for h in range(4):
                pe_mm(W1[:, h * 128:(h + 1) * 128], W_ukT[:, h, :], qnT_all[:, h * 128:(h + 1) * 128], True, True,
                      ["W_ukT", "PT0"], ["W1a"])
            qab_dst = qabsT[:].rearrange("p (h t c) -> p h t c", h=4, t=2)[:, :, tb, :]
            act(qab_dst, W1[:, 0:512].rearrange("p (h c) -> p h c", h=4), AF.Identity, ["W1a"],
                [("qabsT", h, tb) for h in range(4)])
            yield

        def gen_prepQ(q, b):
            yield from gen_prepQ_tile(q, b, 0)
            yield from gen_prepQ_tile(q, b, 1)

        def gen_prep(q, b):
            prepA_tile(q, b, 0)
            prepA_tile(q, b, 1)
            yield
            yield from gen_prepQ(q, b)

        def gen_attn(q, b):
            nk = q.nk
            gpar = (q.gb0 + b) % 2
            ol = olT[gpar]
            for hp in range(2):
                qa = qabsT[:, hp * 512:(hp + 1) * 512]
                qa_keys = [("qabsT", 2 * hp + hl, tb) for hl in range(2) for tb in range(2)]
                qp_keys = [("qpeT2", 2 * hp + hl, tb) for hl in range(2) for tb in range(2)]
                def emit_S(jk):
                    sbk = jk % 2
                    S = W0[:, sbk * 512:(sbk + 1) * 512]
                    skey = "W0" + "ab"[sbk]
                    hf = jk % 2
                    pe_mm(S, ckvT[:, jk * 128:(jk + 1) * 128], qa, True, False, [("ckvT", jk)] + qa_keys, [skey])
                    pe_mm(S, kpeT2[hf * 64:(hf + 1) * 64, (jk // 2) * 128:(jk // 2 + 1) * 128],
                          qpeT2[hf * 64:(hf + 1) * 64, hp * 512:(hp + 1) * 512], False, True,
                          [("kpeT2", jk)] + qp_keys, [skey])
                    act(PT[sbk][:], S, AF.Exp, [skey], ["PT%d" % sbk], scale=SM_SCALE)

                def emit_PV(jk):
                    sbk = jk % 2
                    for hl in range(2):
                        for tb in range(2):
                            oc = tb * 256
                            pe_mm(W1[:, hl * 512 + oc: hl * 512 + oc + 129],
                                  PT[sbk][:, hl * 256 + tb * 128: hl * 256 + (tb + 1) * 128],
                                  ckv_aug[:, jk, 0:129], (jk == 0 and tb == 0), jk == nk - 1,
                                  ["PT%d" % sbk, ("ckv_aug", jk), "ckv_aug_ones"], ["W1" + "ab"[hl]],
                                  skip=True)

                emit_S(0)
                if nk > 1:
                    emit_S(1)
                for jk in range(nk):
                    emit_PV(jk)
                    if jk + 2 < nk:
                        emit_S(jk + 2)
                    yield
                rl_ap = stat[:, 60:64]
                ov = W1[:].rearrange("p (a c) -> p a c", a=4)
                tk.op("dve", lambda: nc.vector.reciprocal(out=rl_ap.unsqueeze(2), in_=ov[:, :, 128:129]),
                      ["W1a", "W1b"], ["rl4"])
                on_all = q_all[:, 0:2, :].rearrange("p a (b c) -> p (a b) c", b=2)
                v_tt("dve", on_all, ov[:, :, 0:128], rl_ap.unsqueeze(2).to_broadcast([128, 4, 128]), ALU.mult,
                     ["W1a", "W1b", "rl4"], ["q_all"])
                for a4 in range(4):
                    pe_tr(Tp[:, a4 * 128:(a4 + 1) * 128], on_all[:, a4, :], ["q_all", "ident"], ["Tp"])
                v_copy("dve", ol[:, hp * 512:(hp + 1) * 512], Tp[:, 0:512], ["Tp"],
                       [("olT", gpar, 2 * hp + hl, tb) for hl in range(2) for tb in range(2)])
                yield

        def gelu2(wkeys, Xf, kx):
            act(Xf[:], W2[:], AF.Square, wkeys, [kx], scale=math.sqrt(C2))
            yield
            v_stt("dve", Xf[:], Xf[:], 1.0, W2[:], ALU.add, ALU.mult, [kx] + wkeys, [kx])
            yield
            act(Xf[:], Xf[:], AF.Tanh, [kx], [kx], scale=C1)
            yield
            v_stt("dve", Xf[:], Xf[:], 1.0, W2[:], ALU.add, ALU.mult, [kx] + wkeys, [kx])
            yield

        def gen_finish_tile(q, b, tb, embed=None):
            i = 2 * b + tb
            j = q.j
            hb = (q.gb0 + b) % 2
            ol = olT[hb]
            lhs = lambda k: hT[hb][:, k, tb * 128:(tb + 1) * 128]
            lkeys = lambda k: (("hT", hb, tb), k)
            xa = xrot[0] % 3
            xf0 = (xrot[0] + 1) % 3
            xf1 = (xrot[0] + 2) % 3
            xrot[0] += 1
            Xa, Xf0, Xf1 = X[xa], X[xf0], X[xf1]
            ka, k0, k1 = "X%d" % xa, "X%d" % xf0, "X%d" % xf1
            WK = ["W2a", "W2b"]
            dma("ldX%d" % xa, "sp", Xa[:], xrows(q, i), [], [ka])
            hT_mm(W2[:, 0:512], ["W2a"], lhs, lkeys, W_in, "W_in", GA0, GA0 + 512)
            hT_mm(W2[:, 512:1024], ["W2b"], lhs, lkeys, W_in, "W_in", GB0, GB0 + 512)
            var = 0 if i == 0 else (2 if i == q.last else 1)
            for g in range(4):
                srcs = []
                if i > 0:
                    srcs.append((i - 1, g * 5 + 3))
                srcs.append((i, g * 5 + var))
                if i < q.last:
                    srcs.append((i + 1, g * 5 + 4))
                for n_, (si, bi) in enumerate(srcs):
                    ui = (si + q.u_off) % 5
                    pe_mm(Mb[:, g * 128:(g + 1) * 128], u_ring[ui][:, g * 128:(g + 1) * 128],
                          bands[:, bi, :], n_ == 0, n_ == len(srcs) - 1,
                          [("u", ui), "bands"], ["Mb"])
            v_copy("dve", dT[:], Mb[:, 0:512], ["Mb"], ["dT"])
            yield
            if embed is not None:
                embed(tb, xf1)
            act(Xf0[:], W2[:], AF.Tanh, WK, [k0], scale=0.5)
            yield
            v_stt("dve", Xf0[:], Xf0[:], 1.0, W2[:], ALU.add, ALU.mult, [k0] + WK, [k0])
            yield
            for h in range(4):
                col = h * 256 + tb * 128
                pe_mm(W2[:, h * 128:(h + 1) * 128], ol[:, col:col + 128], W_uv[:, h, :], True, True,
                      [("olT", hb, h, tb), "W_uv"], ["W2a"])
            for g in range(4):
                pe_mm(W2[:, 512 + g * 128: 512 + (g + 1) * 128], dT[:, g * 128:(g + 1) * 128],
                      W_pool[:, g, :], True, True, ["dT", "W_pool"], ["W2b"])
            v_stt("dve", TB[:], W2[:], 0.5, Xf0[:], ALU.mult, ALU.mult, WK + [k0], ["TB"])
            yield
            for k in range(8):
                pe_tr(Tp[:, k * 128:(k + 1) * 128], TB[:, k * 128:(k + 1) * 128], ["TB", "ident"], ["Tp"])
            evac_mT()
            yield
            for nb in range(2):
                for k in range(8):
                    pe_mm(W2[:, nb * 512:(nb + 1) * 512], mT[:, k, :], W_out[:, k, nb * 512:(nb + 1) * 512],
                          k == 0, k == 7, [("mT", k), "W_out"], ["W2" + "ab"[nb]])
            ms_ap, ms_key = statcol()
            act(TB[:], W2[:], AF.Square, WK, ["TB", ms_key], scale=1.0 / 32.0, accum=ms_ap)
            yield
            r_ap, r_key = rstd_from(ms_ap, ms_key, EPS)
            yield
            v_stt("dve", Xf1[:], W2[:], r_ap, gateP[0][:], ALU.mult, ALU.mult, WK + [r_key, "gateP0"], [k1])
            yield
            v_tt("dve", Xa[:], Xf1[:], Xa[:], ALU.add, [k1, ka], [ka])
            yield
            norm_to_T(Xa[:], [ka], 1.0 / 32.0, TB, "TB")
            yield
            T_to_hT(lambda k: mT[:, k, :], "mT", 1, j, TB, "TB")
            yield
            mk = lambda k: ("mT", k)
            yield
            for nb in range(2):
                hT_mm(W2[:, nb * 512:(nb + 1) * 512], ["W2" + "ab"[nb]], lambda k: mT[:, k, :], mk, W_inc,
                      "W_inc", nb * 512, (nb + 1) * 512)
            yield from gelu2(WK, Xf0, k0)
            yield
            for nb in range(2):
                hT_mm(W2[:, nb * 512:(nb + 1) * 512], ["W2" + "ab"[nb]], lambda k: mT[:, k, :], mk, W_inc,
                      "W_inc", 1024 + nb * 512, 1024 + (nb + 1) * 512)
            yield from gelu2(WK, Xf1, k1)
            yield
            tk.op("dve", lambda: nc.vector.bn_stats(bnst[:, 0, :], Xf1[:, 0:512]), [k1], ["bnst0"])
            yield
            tk.op("dve", lambda: nc.vector.bn_stats(bnst[:, 1, :], Xf1[:, 512:1024]), [k1], ["bnst1"])
            yield
            tk.op("dve", lambda: nc.vector.bn_aggr(bnag[:], bnst[:]), ["bnst0", "bnst1"], ["bnag"])
            yield
            t_ap, t_key = statcol()
            v_ts("pool", t_ap, bnag[:, 1:2], 4.0 * EPS, None, ALU.add, None, ["bnag"], [t_key])
            yield
            r2_ap, r2_key = statcol()
            v_tt("pool", r2_ap, t_ap, mhalf[:], ALU.pow, [t_key, "mhalf"], [r2_key])
            yield
            v_stt("dve", Xf1[:], Xf1[:], bnag[:, 0:1], lng_bc[:], ALU.subtract, ALU.mult, [k1, "bnag", "lng_bc"], [k1])
            yield
            v_stt("dve", TB[:], Xf1[:], r2_ap, lnb_bc[:], ALU.mult, ALU.add, [k1, r2_key, "lnb_bc"], ["TB"])
            yield
            for g in range(4):
                pe_mm(W2[:, g * 256:(g + 1) * 256], W_sT[:, g, :], TB[:, g * 256:(g + 1) * 256], True, True,
                      ["W_sT", "TB"], ["W2" + "ab"[g // 2]])
            for g in range(4):
                v_stt("dve", Xf0[:, g * 256:(g + 1) * 256], W2[:, g * 256:(g + 1) * 256], bsc[:, g:g + 1],
                      Xf0[:, g * 256:(g + 1) * 256], ALU.add, ALU.mult,
                      ["W2" + "ab"[g // 2], "bsc", k0], [k0])
                yield
            for nb in range(2):
                hT_mm(W2[:, nb * 512:(nb + 1) * 512], ["W2" + "ab"[nb]], lambda k: mT[:, k, :], mk, W_inc,
                      "W_inc", 2048 + nb * 512, 2048 + (nb + 1) * 512)
            act(Xf1[:], W2[:], AF.Tanh, WK, [k1], scale=0.5)
            yield
            v_stt("dve", Xf1[:], Xf1[:], 1.0, W2[:], ALU.add, ALU.mult, [k1] + WK, [k1])
            yield
            v_stt("dve", TB[:], Xf0[:], 0.25, Xf1[:], ALU.mult, ALU.mult, [k0, k1], ["TB"])
            yield
            for k in range(8):
                pe_tr(Tp[:, k * 128:(k + 1) * 128], TB[:, k * 128:(k + 1) * 128], ["TB", "ident"], ["Tp"])
            evac_mT()
            yield
            for nb in range(2):
                for k in range(8):
                    pe_mm(W2[:, nb * 512:(nb + 1) * 512], mT[:, k, :], W_outc[:, k, nb * 512:(nb + 1) * 512],
                          k == 0, k == 7, [("mT", k), "W_outc"], ["W2" + "ab"[nb]])
            ms_ap, ms_key = statcol()
            act(TB[:], W2[:], AF.Square, WK, ["TB", ms_key], scale=1.0 / 32.0, accum=ms_ap)
            yield
            r_ap, r_key = rstd_from(ms_ap, ms_key, EPS)
            yield
            v_stt("dve", Xf1[:], W2[:], r_ap, gateP[1][:], ALU.mult, ALU.mult, WK + [r_key, "gateP1"], [k1])
            yield
            v_tt("dve", Xa[:], Xf1[:], Xa[:], ALU.add, [k1, ka], [ka])
            yield
            dma("stX%d" % xa, "pool", q.y[q.row0 + i * 128: q.row0 + (i + 1) * 128, :], Xa[:], [ka], [])
            yield

        def gen_finish(q, b, embed=None):
            yield from gen_finish_tile(q, b, 0, embed)
            yield from gen_finish_tile(q, b, 1, embed)

        def filler_mm(n):
            for _ in range(n):
                pe_mm(W0[:, 0:512], ident[:], W_in[:, 0, 0:512], True, True, ["ident", "W_in"], ["W0a"])

        def interleave(gens, weights=None, fill=0, fill_idx=0):
            gens = list(gens)
            weights = list(weights) if weights else [1] * len(gens)
            alive = [True] * len(gens)
            while any(alive):
                for n_, g in enumerate(gens):
                    if not alive[n_]:
                        continue
                    for _ in range(weights[n_]):
                        try:
                            next(g)
                        except StopIteration:
                            alive[n_] = False
                            break
                if fill and not alive[fill_idx]:
                    filler_mm(fill)
                yield

        def run(g):
            for _ in g:
                pass

        def windowed(gen_list, width):
            pending = list(gen_list)
            active = []
            while pending or active:
                while pending and len(active) < width:
                    active.append(pending.pop(0))
                nxt = []
                for g in active:
                    try:
                        next(g)
                        nxt.append(g)
                    except StopIteration:
                        pass
                active = nxt
                yield

        def gen_front(q):
            yield from windowed([gen_phase1_tile(q, i) for i in range(q.ntiles)], 2)
            yield from gen_prep(q, 0)
            yield from gen_attn(q, 0)

        qs = make_seq(xs, ys, 0, S_T // 128, 0, True)
        qak = [("qabsT", h, tb) for h in range(4) for tb in range(2)]
        qpk = [("qpeT2", h, tb) for h in range(4) for tb in range(2)]
        qs.p1_opts = lambda i: dict(tbsel=[(TB, "TB"), (qpeT2, qpk)][i % 2], junk=(qabsT, qak))
        olk = lambda p: [("olT", p, h, tb) for h in range(4) for tb in range(2)]
        ada1 = gen_ada(1, [(olT[p][:].bitcast(F32), 512, "ldO%d" % p, olk(p)) for p in range(2)])
        rstd_mode[0] = "act"
        run(interleave([windowed([gen_phase1_tile(qs, i) for i in range(qs.ntiles)], 3), ada1], [1, 1]))
        rstd_mode[0] = "pool"
        build_gateP(0, 1)
        run(gen_cache_tiles(qs))
        nblk = qs.ntiles // 2

        def chain(*gs):
            for g in gs:
                yield from g

        run(gen_prep(qs, 0))
        run(gen_attn(qs, 0))
        if nblk > 1:
            run(gen_prep(qs, 1))
        for b in range(1, nblk):
            if b + 1 < nblk:
                emb = lambda tb, xb, b=b: prepA_tile(qs, b + 1, tb, xb=xb, u_bank=(Mb[:, 0:512], "Mb"))
                front = chain(gen_attn(qs, b), gen_prepQ(qs, b + 1))
            else:
                emb = None
                front = gen_attn(qs, b)
            run(interleave([gen_finish(qs, b - 1, emb), front], [1, 1], fill=FILL, fill_idx=1))
        run(interleave([iter(()), gen_finish(qs, nblk - 1)], [1, 1], fill=FILL))
        build_gateP(1, 0)
        build_gateP(1, 1)
        nkt = qs.nk
        if nkt >= 34:
            XP = ckvT[:, 2304:4352].bitcast(F32)
            X.append(XP)
            xp_alias = [("ckvT", jk) for jk in range(18, 34)]
            TBP = ckvT[:, 256:1280]
            tbp_keys = [("ckvT", jk) for jk in range(2, 10)] + ["TBP"]
            tk.op("dve", lambda: nc.vector.memset(XP[:, 0:8], 0.0), [], xp_alias + ["X3"])
            p_opts = dict(xb=3, tbsel=(TBP, tbp_keys))
            overlap = True
        else:
            p_opts = {}
            overlap = False
        seqs = []
        for sq in range(P_SEQ):
            qp = make_seq(xp, yp, sq * P_T, P_T // 128, 1, False, p_row0=sq * P_T, gb0=nblk + sq)
            qp.u_off = 2 * (sq % 2)
            qp.mk_opts = dict(p_opts)
            qp.p1_opts = lambda i, p_opts=p_opts: dict(p_opts)
            seqs.append(qp)
        if overlap:
            run(gen_front(seqs[0]))
            for sq in range(P_SEQ):
                nxt = gen_front(seqs[sq + 1]) if sq + 1 < P_SEQ else iter(())
                run(interleave([nxt, gen_finish(seqs[sq], 0)], [1, 1], fill=FILL))
        else:
            for qp in seqs:
                run(gen_front(qp))
                run(interleave([iter(()), gen_finish(qp, 0)], [1, 1], fill=FILL))

        tk.wait_all("sp")
    return nc


def _host_constants():
    ident = np.eye(128, dtype=np.float32)
    T = 512
    bands = np.zeros((128, 20, 128), np.float32)
    t = np.arange(T)
    for g, w in enumerate(POOL_W):
        lo = np.clip(t - w // 2, 0, T)
        hi = np.clip(t + w - w // 2, 0, T)
        A = np.zeros((T, T), np.float64)
        for tt in range(T):
            A[tt, lo[tt]:hi[tt]] = 1.0 / (hi[tt] - lo[tt])
        I = np.eye(128)
        bands[:, g * 5 + 0, :] = (A[0:128, 0:128] - I).T
        bands[:, g * 5 + 1, :] = (A[128:256, 128:256] - I).T
        bands[:, g * 5 + 2, :] = (A[384:512, 384:512] - I).T
        bands[:, g * 5 + 3, :] = A[128:256, 0:128].T
        bands[:, g * 5 + 4, :] = A[128:256, 256:384].T
    half = 16
    inv = 10000.0 ** (-np.arange(half, dtype=np.float64) / half)
    tt = np.arange(S_T)
    row = (tt // 64).astype(np.float64)
    col = (tt % 64).astype(np.float64)
    ar = row[:, None] * inv[None, :]
    ac = col[:, None] * inv[None, :]
    C = np.concatenate([np.cos(ar), np.cos(ar), np.cos(ac), np.cos(ac)], axis=1)
    S = np.concatenate([-np.sin(ar), np.sin(ar), -np.sin(ac), np.sin(ac)], axis=1)
    rope = np.concatenate([C, S], axis=1).astype(np.float32)
    return ident, bands, rope


_NC_CACHE = {}


def make_in_maps(x_prompt, x_sample, cache_ckv, cache_kpe, c, c_ctx, w_ada, b_ada, norm_pre, norm_post,
                 w_in_ap, q_norm, w_uq, kv_norm, w_ukv, w_pool, pool_scale, w_out_ap,
                 w_in_c, sgu_ln_g, sgu_ln_b, w_s, b_s, w_out_c, ncores=NCORES, p_seq=P_SEQ):
    f = lambda a: np.ascontiguousarray(np.asarray(a, dtype=np.float32))
    x_prompt, x_sample, cache_ckv, cache_kpe, c, c_ctx = map(f, (x_prompt, x_sample, cache_ckv, cache_kpe, c, c_ctx))
    ident, bands, rope = _host_constants()
    s_t = x_sample.shape[1]
    w_ukv4 = f(w_ukv)[0].reshape(128, 4, 256)
    shared = dict(
        w_ada=f(w_ada),
        b_adaT=f(f(b_ada).reshape(2, 3, 8, 128).transpose(3, 0, 1, 2)),
        npre_c=f(f(norm_pre).reshape(2, 8, 128).transpose(2, 0, 1)),
        npost_c=f(f(norm_post).reshape(2, 8, 128).transpose(2, 0, 1)),
        w_in_ap=f(w_in_ap)[0],
        qnorm_c=f(f(q_norm)[0].reshape(2, 128).T),
        w_uq=f(w_uq)[0],
        kvn=f(kv_norm)[0].reshape(1, 128),
        w_ukT=f(w_ukv4[:, :, 0:128].transpose(2, 1, 0)),
        w_uv=f(w_ukv4[:, :, 128:256]),
        w_pool=f(f(w_pool)[0].transpose(1, 0, 2)),
        pscale=f(pool_scale)[0].reshape(1, 512),
        w_out_ap=f(w_out_ap)[0],
        w_in_c=f(w_in_c)[0],
        ln_g=f(sgu_ln_g)[0].reshape(1, D),
        ln_b=f(sgu_ln_b)[0].reshape(1, D),
        w_sT=f(f(w_s)[0].transpose(2, 0, 1)),
        b_s_c=f(f(b_s)[0].T),
        w_out_c=f(w_out_c)[0],
        ident=ident, bands=bands, rope_tab=f(rope[:s_t]),
    )
    in_maps = []
    for core in range(ncores):
        cond = np.stack([c[core], c_ctx], axis=0)
        m = dict(shared)
        m["xs"] = x_sample[core]
        m["xp"] = f(x_prompt[core * p_seq:(core + 1) * p_seq].reshape(p_seq * P_T, D))
        m["cckv"] = cache_ckv[core, 0]
        m["ckpe"] = cache_kpe[core, 0]
        m["condT"] = f(cond.reshape(2, 8, 128).transpose(2, 1, 0))
        in_maps.append(m)
    return in_maps


def assemble(outs, ncores=NCORES, p_seq=P_SEQ):
    y_sample = np.stack([outs[i]["ys"] for i in range(ncores)], axis=0).astype(np.float32)
    y_prompt = np.concatenate([outs[i]["yp"].reshape(p_seq, P_T, D) for i in range(ncores)], axis=0).astype(np.float32)
    new_ckv = np.concatenate([outs[i]["nckv"].reshape(p_seq, 1, P_T, 128) for i in range(ncores)], axis=0).astype(np.float32)
    new_kpe = np.concatenate([outs[i]["nkpe"].reshape(p_seq, 1, P_T, 64) for i in range(ncores)], axis=0).astype(np.float32)
    return (y_prompt, y_sample, new_ckv, new_kpe)


def kernel(**inputs):
    in_maps = make_in_maps(**inputs)
    if "nc" not in _NC_CACHE:
        _NC_CACHE["nc"] = build_program()
    res = run_bass_kernel_spmd(_NC_CACHE["nc"], in_maps, core_ids=list(range(NCORES)))
    return assemble(res.results)
```
